# Optimizing a Trainium2 kernel written in Bass

```python
import jax, jax.numpy as jnp
from jax import lax
import numpy as np

D_MODEL = 4096
BATCH = 2
SEQ = 8192
DEPTH = 1

RWKV_HEAD_DIM = 64
D_RWKV = D_MODEL // 2
RWKV_HEADS = D_RWKV // RWKV_HEAD_DIM
LORA_DECAY = 96
LORA_ICLR = 96
LORA_GATE = 256
D_CONV = D_MODEL // 2
CONV_WIDTH = 3
D_FF = 11008
NORM_EPS = 1e-6
LNX_EPS = 64e-5
N_ADA = 6
N_SHIFT = 3 * D_RWKV + LORA_DECAY + LORA_ICLR + LORA_GATE
N_IN = N_SHIFT + 3 * D_CONV + 2 * D_MODEL

kernel_name = "hybrid_rwkv7_shortconv_convffn_adaln"


def rmsnorm(x, gain):
    xf = x.astype(jnp.float32)
    y = xf * lax.rsqrt(jnp.mean(xf * xf, axis=-1, keepdims=True) + NORM_EPS)
    return (y * gain.astype(jnp.float32)).astype(x.dtype)


def modulate(h, shift, scale):
    return h * (1.0 + scale[:, None, :]) + shift[:, None, :]


def causal_dwconv(x, w):
    seq = x.shape[1]
    xp = jnp.pad(x, ((0, 0), (CONV_WIDTH - 1, 0), (0, 0)))
    return sum(xp[:, k:k + seq, :] * w[k] for k in range(CONV_WIDTH))


def token_shift(p, mu):
    prev = jnp.pad(p, ((0, 0), (1, 0), (0, 0)))[:, :-1, :]
    return p + (prev - p) * mu


def rwkv7_scan(r, decay, k, v, a, b):
    bsz, _, h, n = r.shape

    def step(S, inp):
        r_t, w_t, k_t, v_t, a_t, b_t = inp
        sa = jnp.einsum('bhij,bhj->bhi', S, a_t)
        S = (S * w_t[:, :, None, :] + sa[..., None] * b_t[:, :, None, :]
             + v_t[..., None] * k_t[:, :, None, :])
        return S, jnp.einsum('bhij,bhj->bhi', S, r_t)

    xs = tuple(jnp.moveaxis(t, 1, 0) for t in (r, decay, k, v, a, b))
    S0 = jnp.zeros((bsz, h, n, n), jnp.float32)
    _, ys = lax.scan(step, S0, xs)
    return jnp.moveaxis(ys, 0, 1)


def rwkv7_time_mix(p_r, p_k, p_v, p_wd, p_ad, p_gd, w0, a0, k_k, k_a, r_k,
                   w_lora_decay, w_lora_iclr, w_lora_gate, lnx_w, lnx_b):
    f32 = jnp.float32
    bsz, seq, _ = p_r.shape

    def heads(t):
        return t.astype(f32).reshape(bsz, seq, RWKV_HEADS, RWKV_HEAD_DIM)

    w_log = -jax.nn.softplus(-(w0 + jnp.tanh(p_wd) @ w_lora_decay).astype(f32)) - 0.5
    decay = jnp.exp(-jnp.exp(w_log))
    iclr = jax.nn.sigmoid((a0 + p_ad @ w_lora_iclr).astype(f32))
    g = (jax.nn.sigmoid(p_gd) @ w_lora_gate).astype(f32)
    kk = heads(p_k * k_k)
    kk = kk / jnp.maximum(jnp.sqrt(jnp.sum(kk * kk, axis=-1, keepdims=True)), 1e-12)
    k = p_k.astype(f32) * (1.0 + (iclr - 1.0) * k_a.astype(f32))
    r_h, k_h, v_h = heads(p_r), heads(k), heads(p_v)
    y = rwkv7_scan(r_h, heads(decay), k_h, v_h, -kk, kk * heads(iclr))
    mean = jnp.mean(y, axis=-1, keepdims=True)
    var = jnp.mean(jnp.square(y - mean), axis=-1, keepdims=True)
    y = ((y - mean) * lax.rsqrt(var + LNX_EPS)).reshape(bsz, seq, D_RWKV)
    y = y * lnx_w.astype(f32) + lnx_b.astype(f32)
    bonus = jnp.sum(r_h * k_h * r_k.astype(f32), axis=-1, keepdims=True) * v_h
    o = (y + bonus.reshape(bsz, seq, D_RWKV)) * g
    return o.astype(p_r.dtype)


def token_mixer(h, w_in, mu_shift, w0, a0, k_k, k_a, r_k, w_lora_decay, w_lora_iclr,
                w_lora_gate, lnx_w, lnx_b, conv_w_mix, w_o_rwkv, w_o_conv, w_out):
    proj = h @ w_in
    p_shift = token_shift(proj[..., :N_SHIFT], mu_shift)
    p_conv = proj[..., N_SHIFT:N_SHIFT + 3 * D_CONV]
    p_gate = proj[..., N_SHIFT + 3 * D_CONV:]
    cuts = np.cumsum([D_RWKV, D_RWKV, D_RWKV, LORA_DECAY, LORA_ICLR]).tolist()
    p_r, p_k, p_v, p_wd, p_ad, p_gd = jnp.split(p_shift, cuts, axis=-1)
    y_a = rwkv7_time_mix(p_r, p_k, p_v, p_wd, p_ad, p_gd, w0, a0, k_k, k_a, r_k,
                         w_lora_decay, w_lora_iclr, w_lora_gate, lnx_w, lnx_b) @ w_o_rwkv
    c_b, c_c, c_x = jnp.split(p_conv, 3, axis=-1)
    y_b = (c_b * causal_dwconv(c_c * c_x, conv_w_mix)) @ w_o_conv
    g_a, g_b = jnp.split(p_gate, 2, axis=-1)
    merged = jax.nn.sigmoid(g_a) * y_a + jax.nn.sigmoid(g_b) * y_b
    return merged @ w_out


def channel_mixer(h, w_ffn_up, conv_w_ffn, w_ffn_down):
    u = h @ w_ffn_up
    gate, val = u[..., :D_FF], u[..., D_FF:]
    gate = causal_dwconv(gate, conv_w_ffn)
    return (jax.nn.silu(gate) * val) @ w_ffn_down


def setup_inputs(seed: int = 0) -> dict:
    key = jax.random.key(seed)
    ks = jax.random.split(key, 32)
    f32 = jnp.float32
    nrm = lambda k, shape, s: jax.random.normal(k, shape, f32) * s
    L = DEPTH
    return {
        "x": nrm(ks[0], (BATCH, SEQ, D_MODEL), 1.0),
        "c": nrm(ks[1], (BATCH, D_MODEL), 1.0),
        "w_ada": nrm(ks[2], (L, D_MODEL, N_ADA * D_MODEL), 0.5 * D_MODEL ** -0.5),
        "b_ada": nrm(ks[3], (L, N_ADA * D_MODEL), 0.02),
        "norm1_gain": 1.0 + nrm(ks[4], (L, D_MODEL), 0.02),
        "w_in": nrm(ks[5], (L, D_MODEL, N_IN), D_MODEL ** -0.5),
        "mu_shift": jax.random.uniform(ks[6], (L, N_SHIFT), f32),
        "w0": jax.random.uniform(ks[7], (L, D_RWKV), f32, -6.0, -1.0),
        "a0": nrm(ks[8], (L, D_RWKV), 0.1),
        "k_k": 0.85 + nrm(ks[9], (L, D_RWKV), 0.05),
        "k_a": 1.0 + nrm(ks[10], (L, D_RWKV), 0.05),
        "r_k": nrm(ks[11], (L, RWKV_HEADS, RWKV_HEAD_DIM), 0.1),
        "w_lora_decay": nrm(ks[12], (L, LORA_DECAY, D_RWKV), 0.5 * LORA_DECAY ** -0.5),
        "w_lora_iclr": nrm(ks[13], (L, LORA_ICLR, D_RWKV), 0.5 * LORA_ICLR ** -0.5),
        "w_lora_gate": nrm(ks[14], (L, LORA_GATE, D_RWKV), LORA_GATE ** -0.5),
        "lnx_w": 1.0 + nrm(ks[15], (L, D_RWKV), 0.02),
        "lnx_b": nrm(ks[16], (L, D_RWKV), 0.02),
        "conv_w_mix": nrm(ks[17], (L, CONV_WIDTH, D_CONV), CONV_WIDTH ** -0.5),
        "w_o_rwkv": nrm(ks[18], (L, D_RWKV, D_MODEL), D_RWKV ** -0.5),
        "w_o_conv": nrm(ks[19], (L, D_CONV, D_MODEL), D_CONV ** -0.5),
        "w_out": nrm(ks[20], (L, D_MODEL, D_MODEL), D_MODEL ** -0.5),
        "norm2_gain": 1.0 + nrm(ks[21], (L, D_MODEL), 0.02),
        "w_ffn_up": nrm(ks[22], (L, D_MODEL, 2 * D_FF), D_MODEL ** -0.5),
        "conv_w_ffn": nrm(ks[23], (L, CONV_WIDTH, D_FF), CONV_WIDTH ** -0.5),
        "w_ffn_down": nrm(ks[24], (L, D_FF, D_MODEL), D_FF ** -0.5),
        "final_gain": 1.0 + nrm(ks[25], (D_MODEL,), 0.02),
    }


def reference(x, c, w_ada, b_ada, norm1_gain, w_in, mu_shift, w0, a0, k_k, k_a, r_k,
              w_lora_decay, w_lora_iclr, w_lora_gate, lnx_w, lnx_b, conv_w_mix,
              w_o_rwkv, w_o_conv, w_out, norm2_gain, w_ffn_up, conv_w_ffn, w_ffn_down,
              final_gain):
    c_act = jax.nn.silu(c)
    for layer in range(DEPTH):
        mod = c_act @ w_ada[layer] + b_ada[layer]
        shift1, scale1, gate1, shift2, scale2, gate2 = jnp.split(mod, N_ADA, axis=-1)
        h = modulate(rmsnorm(x, norm1_gain[layer]), shift1, scale1)
        y = token_mixer(h, w_in[layer], mu_shift[layer], w0[layer], a0[layer], k_k[layer],
                        k_a[layer], r_k[layer], w_lora_decay[layer], w_lora_iclr[layer],
                        w_lora_gate[layer], lnx_w[layer], lnx_b[layer], conv_w_mix[layer],
                        w_o_rwkv[layer], w_o_conv[layer], w_out[layer])
        x = x + gate1[:, None, :] * y
        h = modulate(rmsnorm(x, norm2_gain[layer]), shift2, scale2)
        y = channel_mixer(h, w_ffn_up[layer], conv_w_ffn[layer], w_ffn_down[layer])
        x = x + gate2[:, None, :] * y
    return rmsnorm(x, final_gain)
```

```python
import contextlib
import numpy as np
import ml_dtypes
import concourse.bass as bass
import concourse.mybir as mybir
from concourse.bass_utils import run_bass_kernel_spmd

F32 = mybir.dt.float32
BF = mybir.dt.bfloat16
AF = mybir.ActivationFunctionType
OP = mybir.AluOpType
AX = mybir.AxisListType

D = 4096
KC = 32
HD = 64
DR = 2048
DFF = 11008
NFF = 86
NCORE = 8
C0 = float(np.exp(-0.5))
NORM_EPS = 1e-6
LNX_EPS = 64e-5
FFG = [11, 11, 11, 11, 11, 11, 10, 10]


class Res:
    __slots__ = ("w", "r", "name")

    def __init__(self, name=""):
        self.w = None
        self.r = {}
        self.name = name


class Sched:
    def __init__(self, nc, stack):
        self.nc = nc
        self.stack = stack
        self.eng = {"pe": nc.tensor, "act": nc.scalar, "dve": nc.vector, "pool": nc.gpsimd,
                    "sp": nc.sync}
        self.sem = {}
        self.cnt = {}
        self.isdma = set()
        self.known = {e: {} for e in self.eng}
        for e in ("pe", "act", "dve", "pool"):
            self.newsem(e)

    def newsem(self, key, dma=False):
        self.sem[key] = self.stack.enter_context(self.nc.semaphore("s_" + key))
        self.cnt[key] = 0
        if dma:
            self.isdma.add(key)

    def _waits(self, e, reads, writes):
        need = {}
        for R in reads:
            if R.w is not None:
                need[R.w[0]] = max(need.get(R.w[0], 0), R.w[1])
        for W in writes:
            if W.w is not None:
                need[W.w[0]] = max(need.get(W.w[0], 0), W.w[1])
            for k, v in W.r.items():
                need[k] = max(need.get(k, 0), v)
        for k, v in need.items():
            if k in self.isdma:
                v = self.cnt[k]
            if k == e and e == "pe":
                continue
            if self.known[e].get(k, 0) < v:
                self.eng[e].wait_ge(self.sem[k], v)
                self.known[e][k] = v

    def _mark(self, key, val, reads, writes):
        for W in writes:
            W.w = (key, val)
            W.r = {}
        for R in reads:
            if R.r.get(key, 0) < val:
                R.r[key] = val

    def op(self, e, fn, reads=(), writes=()):
        self._waits(e, reads, writes)
        ins = fn(self.eng[e])
        self.cnt[e] += 1
        ins.then_inc(self.sem[e], 1)
        self._mark(e, self.cnt[e], reads, writes)

    def dma(self, q, fn, semkey, reads=(), writes=()):
        self._waits(q, reads, writes)
        ins = fn(self.eng[q])
        self.cnt[semkey] += 16
        ins.then_inc(self.sem[semkey], 16)
        self._mark(semkey, self.cnt[semkey], reads, writes)

    def wait_all(self, e, keys):
        for k in keys:
            v = self.cnt[k]
            if v and self.known[e].get(k, 0) < v:
                self.eng[e].wait_ge(self.sem[k], v)
                self.known[e][k] = v


class WStream:
    def __init__(self, S, nc, stack, nslots, slot_elems):
        self.S = S
        self.n = nslots
        self.buf = stack.enter_context(nc.sbuf_tensor("wring", [128, nslots, slot_elems], BF))
        self.res = [Res("w%d" % i) for i in range(nslots)]
        self.plan = []
        self.issued = 0
        self.used = 0
        for i in range(nslots):
            S.newsem("wd%d" % i, dma=True)

    def add(self, ap, E):
        self.plan.append((ap, E))

    def _issue(self):
        i = self.issued
        ap, E = self.plan[i]
        s = i % self.n
        dst = self.buf[:, s, 0:E]
        if E > 2048 and E % 2048 == 0:
            src = ap.rearrange("p (a b) -> p a b", b=2048)
            dst = dst.rearrange("p (a b) -> p a b", b=2048)
        else:
            src = ap
        self.S.dma("pool", lambda g: g.dma_start(out=dst, in_=src), "wd%d" % s,
                   reads=(), writes=(self.res[s],))
        self.issued += 1

    def next(self):
        i = self.used
        while self.issued < min(i + self.n, len(self.plan)):
            self._issue()
        self.used += 1
        s = i % self.n
        return self.buf[:, s, :], self.res[s]


STOP = None


def build(S, dbg=False, phase2=True):
    TQ = S // 4
    TT2 = min(256, TQ)
    TT = min(256, TQ)
    NT1 = S // TT
    NT2 = TQ // TT2
    NSUB = TT // 128
    NCH = TT // 64
    nc = bass.Bass("TRN2", target_bir_lowering=False)
    stack = contextlib.ExitStack()
    Sx = Sched(nc, stack)

    def din(name, shape, dt=F32):
        return nc.dram_tensor(name, list(shape), dt, kind="ExternalInput").ap()

    x_d = din("x", [S, D])
    cT_d = din("cT", [128, KC])
    wada_d = din("wada", [192, 128, KC * 128])
    vec0_d = din("vec0", [128, 192 + 96])
    w1_d = din("w1", [16, 128, KC * 128])
    vec1_d = din("vec1", [128, 36])
    lnx_d = din("lnx", [128, 1024])
    wld_d = din("wld", [96, 512])
    wli_d = din("wli", [96, 512])
    wlg_d = din("wlg", [128, 2 * 512])
    cst_d = din("cst", [128, 2176])
    vec2_d = din("vec2", [128, 48 + 3 * NFF + 32])
    if phase2:
        x2_d = din("x2", [4 + TQ, D])
        w2c_d = din("w2c", [48, 128, KC * 128])
        w2g_d = din("w2g", [64, 128, KC * 128])
        wor_d = din("wor", [32, 128, 16 * 128])
        woc_d = din("woc", [32, 128, 16 * 128])
        wout_d = din("wout", [32, 128, KC * 128])
        wup_d = din("wup", [172, 128, KC * 128])
        wdn_d = din("wdn", [32, 128, NFF * 128])
    out_d = nc.dram_tensor("out", [TQ, D], F32, kind="ExternalOutput").ap()
    oT_loc = [nc.dram_tensor("oTloc%d" % q, [512, TQ // 2], F32) for q in range(4)]
    og = [nc.dram_tensor("og%d" % q, [2 * DR, TQ // 2], F32) for q in range(4)]
    if dbg:
        dbg_o = nc.dram_tensor("dbg_o", [512, S], BF, kind="ExternalOutput").ap()
        dbg_mod = nc.dram_tensor("dbg_mod", [128, 192], F32, kind="ExternalOutput").ap()

    cur = [stack]

    def sb(name, shape, dt=F32):
        return cur[0].enter_context(nc.sbuf_tensor("sb_" + name, list(shape), dt))

    def finish():
        Sx.wait_all("sp", ["st_o"])
        Sx.wait_all("pool", ["pe", "act", "dve", "st_o"])
        return nc, stack

    def barrier():
        keys = list(Sx.sem.keys())
        for e in ("pe", "act", "dve", "pool", "sp"):
            Sx.wait_all(e, keys)

    PB = [stack.enter_context(nc.psum_tensor("pb%d" % i, [128, 512], F32)) for i in range(8)]
    PR = [Res("pb%d" % i) for i in range(8)]

    def pbf(i):
        return PB[i][:, :].bitcast(BF)

    for k in ("ld_c", "ld_x0", "ld_x1", "st_o", "ld_o", "cc"):
        Sx.newsem(k, dma=True)
    cst = sb("cst", [128, 2176])
    r_cst = Res("cst")
    vec0 = sb("vec0", [128, 288])
    vec1 = sb("vec1", [128, 36])
    vec2 = sb("vec2", [128, 48 + 3 * NFF + 32])
    lnx = sb("lnx", [128, 1024])
    cT = sb("cT", [128, KC])
    big1 = sb("big1", [128, 4096])
    wld32 = big1[0:96, 0:512]
    wli32 = big1[0:96, 512:1024]
    wlg32 = big1[:, 1024:2048]
    for dst, src in ((cst[:, :], cst_d), (vec0[:, :], vec0_d), (vec1[:, :], vec1_d), (vec2[:, :], vec2_d), (lnx[:, :], lnx_d),
                     (cT[:, :], cT_d), (wld32, wld_d), (wli32, wli_d), (wlg32, wlg_d)):
        Sx.dma("sp", lambda g, dst=dst, src=src: g.dma_start(out=dst, in_=src), "ld_c",
               writes=(r_cst,))
    identF = cst[:, 0:128]
    maskGT = cst[:, 128:640]
    maskL8 = cst[0:64, 640:1152]
    ident8 = cst[0:64, 1152:1664]
    cmask = cst[:, 1664:1664 + TT]
    cb = sb("cb", [128, 512], BF)
    r_cb = Res("cb")
    Sx.op("dve", lambda e: e.tensor_copy(out=cb[:, 0:128], in_=identF), reads=(r_cst,), writes=(r_cb,))
    Sx.op("dve", lambda e: e.memset(cb[:, 128:256], 1.0), writes=(r_cb,))
    Sx.op("dve", lambda e: e.memset(cb[:, 256:386], 0.0), writes=(r_cb,))
    Sx.op("dve", lambda e: e.memset(cb[0:64, 256:320], 1.0), writes=(r_cb,))
    Sx.op("dve", lambda e: e.memset(cb[64:128, 320:384], 1.0), writes=(r_cb,))
    Sx.op("dve", lambda e: e.memset(cb[0:64, 384:385], 1.0), writes=(r_cb,))
    Sx.op("dve", lambda e: e.memset(cb[64:128, 385:386], 1.0), writes=(r_cb,))
    identB = cb[:, 0:128]
    onesB = cb[:, 128:256]
    blkones = cb[:, 256:384]
    hind = cb[:, 384:386]
    wld = sb("wld", [96, 512], BF)
    wli = sb("wli", [96, 512], BF)
    wlg = sb("wlg", [128, 1024], BF)
    Sx.op("dve", lambda e: e.tensor_copy(out=wld[:, :], in_=wld32), reads=(r_cst,), writes=(r_cb,))
    Sx.op("dve", lambda e: e.tensor_copy(out=wli[:, :], in_=wli32), reads=(r_cst,), writes=(r_cb,))
    Sx.op("dve", lambda e: e.tensor_copy(out=wlg[:, :], in_=wlg32), reads=(r_cst,), writes=(r_cb,))
    epsn = sb("epsn", [128, 2])
    Sx.op("dve", lambda e: e.memset(epsn[:, 0:1], NORM_EPS), writes=(r_cb,))
    Sx.op("dve", lambda e: e.memset(epsn[:, 1:2], LNX_EPS), writes=(r_cb,))

    WS = WStream(Sx, nc, stack, 4, KC * 128)
    for n in range(192):
        WS.add(wada_d[n], KC * 128)
    for t in range(NT1):
        for c in range(16):
            WS.add(w1_d[c], KC * 128)

    def p2_plan(halo):
        for m in range(16):
            for k in range(3):
                WS.add(w2c_d[3 * m + k], KC * 128)
        for n in range(32):
            WS.add(w2g_d[2 * n], KC * 128)
            WS.add(w2g_d[2 * n + 1], KC * 128)
            WS.add(wor_d[n], 16 * 128)
            WS.add(woc_d[n], 16 * 128)
        for n in range(32):
            WS.add(wout_d[n], KC * 128)
        m0 = 0
        for gsz in FFG:
            for m in range(m0, m0 + gsz):
                WS.add(wup_d[m], KC * 128)
                if not halo:
                    WS.add(wup_d[NFF + m], KC * 128)
            if not halo:
                for nn in range(32):
                    WS.add(wdn_d[nn][:, m0 * 128:(m0 + gsz) * 128], gsz * 128)
            m0 += gsz
    if phase2:
        p2_plan(True)
        for t in range(NT2):
            p2_plan(False)

    cact = sb("cact", [128, KC, 2], BF)
    r_cact = Res("cact")
    for j2 in range(2):
        Sx.op("act", lambda e, j2=j2: e.activation(out=cact[:, :, j2], in_=cT[:, :], func=AF.Silu),
              reads=(r_cst,), writes=(r_cact,))
    for n in range(192):
        wt, wr = WS.next()
        wt3 = wt.rearrange("p (k c) -> p k c", c=128)
        for kc in range(KC):
            Sx.op("pe", lambda e, n=n, kc=kc, wt3=wt3: e.matmul(
                PB[0][:, 2 * n:2 * n + 2], lhsT=wt3[:, kc, :], rhs=cact[:, kc, :],
                start=(kc == 0), stop=(kc == KC - 1)), reads=(wr, r_cact), writes=(PR[0],))
    modT = sb("modT", [128, 192])
    r_mod = Res("mod")
    Sx.op("dve", lambda e: e.tensor_tensor(
        out=modT[:, :], in0=PB[0][:, 0:384].rearrange("p (n two) -> p n two", two=2)[:, :, 0],
        in1=vec0[:, 0:192], op=OP.add), reads=(PR[0], r_cst), writes=(r_mod,))
    gsh = sb("gsh", [128, 64])
    for (o, sc, gn) in ((0, 32, 192), (32, 128, 224)):
        Sx.op("dve", lambda e, o=o, sc=sc: e.tensor_scalar(
            out=gsh[:, o:o + 32], in0=modT[:, sc:sc + 32], scalar1=1.0, scalar2=None, op0=OP.add),
            reads=(r_mod,), writes=(r_mod,))
        Sx.op("dve", lambda e, o=o, gn=gn: e.tensor_tensor(
            out=gsh[:, o:o + 32], in0=gsh[:, o:o + 32], in1=vec0[:, gn:gn + 32], op=OP.mult),
            reads=(r_mod, r_cst), writes=(r_mod,))
    g1, sh1, gate1 = gsh[:, 0:32], modT[:, 0:32], modT[:, 64:96]
    g2, sh2, gate2 = gsh[:, 32:64], modT[:, 96:128], modT[:, 160:192]
    fgain = vec0[:, 256:288]
    if dbg:
        Sx.dma("sp", lambda g: g.dma_start(out=dbg_mod, in_=modT[:, :]), "st_o", reads=(r_mod,))

    if STOP == "phase0":
        return finish()
    xs_buf = [sb("xs%d" % i, [128, D // 2]) for i in range(2)]
    xs_res = [Res("xs%d" % i) for i in range(2)]
    tpbuf = sb("tpbuf", [128, 10 * 512])
    r_sq = Res("tpA")
    sq = sb("sq", [128, KC, 128], BF)
    rstd = sb("rstd", [128, 128])
    r_rstd = Res("rstd")
    ntmp = [sb("ntmp%d" % i, [128, 128]) for i in range(2)]
    r_ntmp = [Res("ntmp%d" % i) for i in range(2)]
    norm_ctr = [0]

    def norm_sub(x_rows_ap, rows, xT_dst, r_xT, hT_dst, r_hT, g, sh):
        for grp in range(8):
            b = grp % 2
            if grp % 4 == 0:
                i = norm_ctr[0] % 2
                norm_ctr[0] += 1
                xs, xr = xs_buf[i], xs_res[i]
                hf = grp // 4
                Sx.dma("sp", lambda q, xs=xs, hf=hf: q.dma_start(
                    out=xs[0:rows, :], in_=x_rows_ap[:, hf * 2048:(hf + 1) * 2048]), "ld_x%d" % i, writes=(xr,))
            for k4 in range(4):
                n = grp * 4 + k4
                nl = n % 16
                Sx.op("pe", lambda e, nl=nl, k4=k4, b=b, xs=xs: e.transpose(
                    out=PB[b][:, k4 * 128:k4 * 128 + rows], in_=xs[0:rows, nl * 128:(nl + 1) * 128],
                    identity=identF[0:rows, 0:rows]), reads=(xr, r_cst), writes=(PR[b],))
            if STOP == "n1":
                continue
            src = PB[b][:, :].rearrange("p (a t) -> p a t", a=4)[:, :, 0:rows]
            Sx.op("dve", lambda e, grp=grp, src=src: e.tensor_copy(
                out=xT_dst[:, grp * 4:grp * 4 + 4, 0:rows], in_=src), reads=(PR[b],), writes=(r_xT,))
            if STOP == "n2":
                continue
            Sx.op("act", lambda e, grp=grp: e.activation(
                out=sq[:, grp * 4:grp * 4 + 4, 0:rows], in_=xT_dst[:, grp * 4:grp * 4 + 4, 0:rows], func=AF.Square),
                reads=(r_xT,), writes=(r_sq,))
        if STOP in ("n1", "n2", "n3"):
            return
        for n in range(KC):
            Sx.op("pe", lambda e, n=n: e.matmul(PB[2][:, 0:rows], lhsT=onesB, rhs=sq[:, n, 0:rows],
                                                start=(n == 0), stop=(n == KC - 1)),
                  reads=(r_sq, r_cb), writes=(PR[2],))
        if STOP == "n4":
            return
        Sx.op("act", lambda e: e.activation(out=rstd[:, 0:rows], in_=PB[2][:, 0:rows], func=AF.Sqrt,
                                            bias=epsn[:, 0:1], scale=1.0 / D),
              reads=(PR[2], r_cb), writes=(r_rstd,))
        Sx.op("dve", lambda e: e.reciprocal(out=rstd[:, 0:rows], in_=rstd[:, 0:rows]),
              reads=(r_rstd,), writes=(r_rstd,))
        if STOP == "n5":
            return
        for n in range(KC):
            t = ntmp[n % 2]
            tr = r_ntmp[n % 2]
            Sx.op("dve", lambda e, n=n, t=t: e.scalar_tensor_tensor(
                out=t[:, 0:rows], in0=xT_dst[:, n, 0:rows], scalar=g[:, n:n + 1], in1=rstd[:, 0:rows],
                op0=OP.mult, op1=OP.mult), reads=(r_xT, r_rstd, r_mod), writes=(tr,))
            Sx.op("act", lambda e, n=n, t=t: e.activation(
                out=hT_dst[:, n, 0:rows], in_=t[:, 0:rows], func=AF.Identity, bias=sh[:, n:n + 1],
                scale=1.0), reads=(tr, r_mod), writes=(r_hT,))

    st1 = contextlib.ExitStack()
    cur[0] = st1
    hT = sb("hT", [128, KC, TT], BF)
    r_hT = Res("hT")
    r_proj = Res("proj")
    xTs = big1[:, :].rearrange("p (k t) -> p k t", t=128)
    r_xTs = r_proj
    carry = sb("carry", [128, 16])
    r_carry = Res("carry")
    Sx.op("dve", lambda e: e.memset(carry[:, :], 0.0), writes=(r_carry,))
    Pb = [sb("Pb%d" % i, [128, TT + 1]) for i in range(2)]
    r_Pb = [Res("Pb%d" % i) for i in range(2)]
    dtmp = sb("dtmp", [128, TT])
    r_dtmp = Res("dtmp")
    rT = big1[:, 0:4 * TT].rearrange("p (c t) -> p c t", c=4)
    kT = big1[:, 2048:2048 + 4 * TT].rearrange("p (c t) -> p c t", c=4)
    vpad = sb("vpad", [128, 4, 64 + TT], BF)
    twd = sb("twd", [96, TT], BF)
    adT = sb("adT", [96, TT], BF)
    sgd = sb("sgd", [128, 2, 64 + TT], BF)
    rkT = sb("rkT", [128, 4, 64 + TT], BF)
    Sx.op("dve", lambda e: e.memset(vpad[:, :, 0:64], 0.0), writes=(r_proj,))
    Sx.op("dve", lambda e: e.memset(sgd[:, :, 0:64], 0.0), writes=(r_proj,))
    AR = sb("AR", [128, 4, NCH, 128], BF)
    BK = sb("BK", [128, 4, NCH, 128], BF)
    BKh = sb("BKh", [128, 4, NCH, 128], BF)
    Wc = sb("Wc", [128, 4, NCH])
    ARo = sb("ARo", [64, 4, NCH, 128], BF)
    BKo = sb("BKo", [64, 4, NCH, 128], BF)
    BKho = sb("BKho", [64, 4, NCH, 128], BF)
    vo = sb("vo", [64, 4, TT], BF)
    rko = sb("rko", [64, 4, TT], BF)
    Wco = sb("Wco", [64, 4, NCH])
    r_sh = Res("shift")
    Sx.newsem("sh", dma=True)
    r_prep = Res("prep")
    Sx.op("dve", lambda e: e.memset(rkT[:, :, 0:64], 0.0), writes=(r_prep,))
    tnames = ["sg", "Ls", "Er", "Ek", "Ea", "ic", "kq", "kkv", "kp", "t1"]
    tp = {n_: tpbuf[:, i_ * 512:i_ * 512 + TT] for i_, n_ in enumerate(tnames)}
    r_tp = {n_: (r_sq if i_ < 4 else Res("tp_" + n_)) for i_, n_ in enumerate(tnames)}
    sqk = sb("sqk", [128, TT], BF)
    r_sqk = Res("sqk")
    M32 = sb("M32", [64, 8, 64])
    Mb = sb("Mb", [64, 8, 64], BF)
    r_M = Res("M")
    Sx.op("dve", lambda e: e.memset(M32[:, :, :], 0.0), writes=(r_M,))
    Sx.op("dve", lambda e: e.memset(Mb[:, :, :], 0.0), writes=(r_M,))
    GTb = sb("GTb", [64, 8, 128], BF)
    GTk = sb("GTk", [64, 8, 128], BF)
    r_GTm = Res("GTm")
    Xb = [sb("Xb%d" % i, [64, 8, 64]) for i in range(2)]
    Xtb = [sb("Xtb%d" % i, [64, 8, 64]) for i in range(2)]
    r_Xb = [Res("Xb%d" % i) for i in range(2)]
    r_Xtb = [Res("Xtb%d" % i) for i in range(2)]
    Tt = sb("Tt", [64, 8, 64])
    r_Tt = Res("Tt")
    Psb = sb("Psb", [64, 8, 64])
    r_Psb = Res("Psb")
    Ut = sb("Ut", [64, 8, 64], BF)
    Vt = sb("Vt", [64, 8, 64], BF)
    BhT = sb("BhT", [64, 8, 64], BF)
    KhT = sb("KhT", [64, 8, 64], BF)
    r_U = Res("U")
    r_V = Res("V")
    r_BKhT = Res("BKhT")
    ysb = sb("ysb", [128, 8, 64])
    ysq = sb("ysq", [128, 8, 64])
    r_y = Res("y")
    st = sb("st", [128, 64])
    r_st = Res("st")
    ob = sb("ob", [128, 512], BF)
    r_ob = Res("ob")
    oTt = [sb("oTt%d" % i, [128, 4, TT], BF) for i in range(1)]
    r_oTt = [Res("oTt%d" % i) for i in range(1)]

    mu = vec1[:, 0:16]
    w0v, a0v, kkv_, kav, rkv = (vec1[:, 16 + 4 * i:20 + 4 * i] for i in range(5))

    def v3(ap, a):
        return ap.rearrange("p (a b) -> p a b", a=a)

    for t in range(NT1):
        for s in range(NSUB):
            r0 = t * TT + s * 128
            norm_sub(x_d[r0:r0 + 128, :], 128, xTs, r_xTs, hT[:, :, s * 128:(s + 1) * 128], r_hT, g1, sh1)
        if STOP in ("norm", "n1", "n2", "n3", "n4", "n5"):
            return finish()
        for c in range(16):
            wt, wr = WS.next()
            wt3 = wt.rearrange("p (k c) -> p k c", c=128)
            Mw = 96 if c in (12, 13) else 128
            b = 3 + (c % 2)
            for kc in range(KC):
                Sx.op("pe", lambda e, kc=kc, wt3=wt3, Mw=Mw, b=b: e.matmul(
                    PB[b][0:Mw, 0:TT], lhsT=wt3[:, kc, 0:Mw], rhs=hT[:, kc, :],
                    start=(kc == 0), stop=(kc == KC - 1)), reads=(wr, r_hT), writes=(PR[b],))
            P_, rP = Pb[c % 2], r_Pb[c % 2]
            Sx.op("act", lambda e, P_=P_, Mw=Mw, b=b: e.activation(
                out=P_[0:Mw, 1:TT + 1], in_=PB[b][0:Mw, 0:TT], func=AF.Copy),
                reads=(PR[b],), writes=(rP,))
            Sx.op("dve", lambda e, P_=P_, Mw=Mw, c=c: e.tensor_copy(
                out=P_[0:Mw, 0:1], in_=carry[0:Mw, c:c + 1]), reads=(r_carry,), writes=(rP,))
            Sx.op("dve", lambda e, P_=P_, Mw=Mw: e.tensor_tensor(
                out=dtmp[0:Mw, :], in0=P_[0:Mw, 0:TT], in1=P_[0:Mw, 1:TT + 1], op=OP.subtract),
                reads=(rP,), writes=(r_dtmp,))
            Sx.op("dve", lambda e, P_=P_, Mw=Mw, c=c: e.tensor_copy(
                out=carry[0:Mw, c:c + 1], in_=P_[0:Mw, TT:TT + 1]), reads=(rP,), writes=(r_carry,))
            if c < 4:
                dst = rT[:, c, :]
            elif c < 8:
                dst = kT[:, c - 4, :]
            elif c < 12:
                dst = vpad[:, c - 8, 64:64 + TT]
            elif c == 12:
                dst = tp["t1"][0:96, :]
            elif c == 13:
                dst = adT[0:96, :]
            else:
                dst = tp["t1"]
            wres = (r_proj,) if c not in (12, 14, 15) else (r_tp["t1"],)
            Sx.op("dve", lambda e, P_=P_, Mw=Mw, c=c, dst=dst: e.scalar_tensor_tensor(
                out=dst, in0=dtmp[0:Mw, :], scalar=mu[0:Mw, c:c + 1], in1=P_[0:Mw, 1:TT + 1],
                op0=OP.mult, op1=OP.add), reads=(r_dtmp, rP, r_cst), writes=wres)
            if c == 12:
                Sx.op("act", lambda e: e.activation(out=twd[0:96, :], in_=tp["t1"][0:96, :], func=AF.Tanh),
                      reads=(r_tp["t1"],), writes=(r_proj,))
            if c in (14, 15):
                Sx.op("act", lambda e, c=c: e.activation(out=sgd[:, c - 14, 64:64 + TT], in_=tp["t1"],
                                                         func=AF.Sigmoid),
                      reads=(r_tp["t1"],), writes=(r_proj,))
        if STOP == "proj":
            return finish()
        for cc in range(4):
            cs = slice(cc * 128, (cc + 1) * 128)
            T_ = tp
            Sx.op("pe", lambda e, cs=cs: e.matmul(PB[5][:, 0:TT], lhsT=wld[0:96, cs], rhs=twd[0:96, :],
                                                  start=True, stop=True),
                  reads=(r_cb, r_proj), writes=(PR[5],))
            Sx.op("act", lambda e, cc=cc: e.activation(out=T_["sg"], in_=PB[5][:, 0:TT], func=AF.Sigmoid,
                                                       bias=w0v[:, cc:cc + 1], scale=1.0),
                  reads=(PR[5], r_cst), writes=(r_tp["sg"],))
            Sx.op("dve", lambda e: e.tensor_tensor_scan(out=T_["Ls"], data0=cmask, data1=T_["sg"],
                                                        initial=0.0, op0=OP.mult, op1=OP.add),
                  reads=(r_tp["sg"], r_cst), writes=(r_tp["Ls"],))
            Sx.op("act", lambda e: e.activation(out=T_["Er"], in_=T_["Ls"], func=AF.Exp, scale=-C0),
                  reads=(r_tp["Ls"],), writes=(r_tp["Er"],))
            Sx.op("act", lambda e: e.activation(out=T_["Ek"], in_=T_["Ls"], func=AF.Exp, scale=C0),
                  reads=(r_tp["Ls"],), writes=(r_tp["Ek"],))
            Sx.op("dve", lambda e: e.tensor_tensor(out=T_["t1"], in0=T_["Ls"], in1=T_["sg"],
                                                   op=OP.subtract),
                  reads=(r_tp["Ls"], r_tp["sg"]), writes=(r_tp["t1"],))
            Sx.op("act", lambda e: e.activation(out=T_["Ea"], in_=T_["t1"], func=AF.Exp, scale=-C0),
                  reads=(r_tp["t1"],), writes=(r_tp["Ea"],))
            Sx.op("dve", lambda e, cc=cc: e.tensor_copy(
                out=Wc[:, cc, :], in_=v3(T_["Er"], NCH)[:, :, 63]), reads=(r_tp["Er"],), writes=(r_prep,))
            Sx.op("pe", lambda e, cs=cs: e.matmul(PB[6][:, 0:TT], lhsT=wli[0:96, cs], rhs=adT[0:96, :],
                                                  start=True, stop=True),
                  reads=(r_cb, r_proj), writes=(PR[6],))
            Sx.op("act", lambda e, cc=cc: e.activation(out=T_["ic"], in_=PB[6][:, 0:TT], func=AF.Sigmoid,
                                                       bias=a0v[:, cc:cc + 1], scale=1.0),
                  reads=(PR[6], r_cst), writes=(r_tp["ic"],))
            Sx.op("dve", lambda e, cc=cc: e.tensor_scalar(out=T_["kq"], in0=kT[:, cc, :],
                                                          scalar1=kkv_[:, cc:cc + 1], scalar2=None, op0=OP.mult),
                  reads=(r_proj, r_cst), writes=(r_tp["kq"],))
            Sx.op("act", lambda e: e.activation(out=sqk[:, :], in_=T_["kq"], func=AF.Square),
                  reads=(r_tp["kq"],), writes=(r_sqk,))
            Sx.op("pe", lambda e: e.matmul(PB[7][:, 0:TT], lhsT=blkones, rhs=sqk[:, :], start=True, stop=True),
                  reads=(r_cb, r_sqk), writes=(PR[7],))
            Sx.op("act", lambda e: e.activation(out=T_["kkv"], in_=PB[7][:, 0:TT], func=AF.Sqrt),
                  reads=(PR[7],), writes=(r_tp["kkv"],))
            Sx.op("dve", lambda e: e.tensor_scalar(out=T_["kkv"], in0=T_["kkv"], scalar1=1e-12,
                                                   scalar2=None, op0=OP.max),
                  reads=(r_tp["kkv"],), writes=(r_tp["kkv"],))
            Sx.op("dve", lambda e: e.reciprocal(out=T_["kkv"], in_=T_["kkv"]),
                  reads=(r_tp["kkv"],), writes=(r_tp["kkv"],))
            Sx.op("dve", lambda e: e.tensor_tensor(out=T_["kkv"], in0=T_["kkv"], in1=T_["kq"],
                                                   op=OP.mult),
                  reads=(r_tp["kkv"], r_tp["kq"]), writes=(r_tp["kkv"],))
            Sx.op("dve", lambda e, cc=cc: e.tensor_scalar(out=T_["t1"], in0=T_["ic"], scalar1=-1.0,
                                                          scalar2=kav[:, cc:cc + 1], op0=OP.add, op1=OP.mult),
                  reads=(r_tp["ic"], r_cst), writes=(r_tp["t1"],))
            Sx.op("dve", lambda e, cc=cc: e.scalar_tensor_tensor(out=T_["kp"], in0=T_["t1"], scalar=1.0,
                                                                 in1=kT[:, cc, :], op0=OP.add, op1=OP.mult),
                  reads=(r_tp["t1"], r_proj), writes=(r_tp["kp"],))
            Sx.op("dve", lambda e, cc=cc: e.scalar_tensor_tensor(
                out=rkT[:, cc, 64:64 + TT], in0=rT[:, cc, :], scalar=rkv[:, cc:cc + 1], in1=T_["kp"],
                op0=OP.mult, op1=OP.mult), reads=(r_proj, r_tp["kp"], r_cst), writes=(r_prep,))
            Sx.op("dve", lambda e, cc=cc: e.scalar_tensor_tensor(
                out=AR[:, cc, :, 0:64], in0=v3(T_["kkv"], NCH), scalar=-1.0, in1=v3(T_["Ea"], NCH),
                op0=OP.mult, op1=OP.mult), reads=(r_tp["kkv"], r_tp["Ea"]), writes=(r_prep,))
            Sx.op("dve", lambda e, cc=cc: e.tensor_tensor(
                out=AR[:, cc, :, 64:128], in0=v3(rT[:, cc, :], NCH), in1=v3(T_["Er"], NCH), op=OP.mult),
                reads=(r_proj, r_tp["Er"]), writes=(r_prep,))
            Sx.op("dve", lambda e: e.tensor_tensor(out=T_["t1"], in0=T_["kkv"], in1=T_["ic"],
                                                   op=OP.mult),
                  reads=(r_tp["kkv"], r_tp["ic"]), writes=(r_tp["t1"],))
            Sx.op("dve", lambda e, cc=cc: e.tensor_tensor(
                out=BK[:, cc, :, 0:64], in0=v3(T_["t1"], NCH), in1=v3(T_["Ek"], NCH), op=OP.mult),
                reads=(r_tp["t1"], r_tp["Ek"]), writes=(r_prep,))
            Sx.op("dve", lambda e, cc=cc: e.tensor_tensor(
                out=BK[:, cc, :, 64:128], in0=v3(T_["kp"], NCH), in1=v3(T_["Ek"], NCH), op=OP.mult),
                reads=(r_tp["kp"], r_tp["Ek"]), writes=(r_prep,))
            for c in range(NCH):
                Sx.op("dve", lambda e, cc=cc, c=c: e.tensor_scalar(
                    out=BKh[:, cc, c, :], in0=BK[:, cc, c, :], scalar1=Wc[:, cc, c:c + 1], scalar2=None,
                    op0=OP.mult), reads=(r_prep,), writes=(r_prep,))
        if STOP == "prep":
            return finish()
        for dst_, src_ in ((ARo, AR), (BKo, BK), (BKho, BKh)):
            Sx.dma("sp", lambda g, dst_=dst_, src_=src_: g.dma_start(out=dst_[:, :, :, :], in_=src_[64:128, :, :, :]),
                   "sh", reads=(r_prep,), writes=(r_sh,))
        Sx.dma("sp", lambda g: g.dma_start(out=vo[:, :, :], in_=vpad[64:128, :, 64:64 + TT]), "sh",
               reads=(r_proj,), writes=(r_sh,))
        Sx.dma("sp", lambda g: g.dma_start(out=rko[:, :, :], in_=rkT[64:128, :, 64:64 + TT]), "sh",
               reads=(r_prep,), writes=(r_sh,))
        Sx.dma("sp", lambda g: g.dma_start(out=Wco[:, :, :], in_=Wc[64:128, :, :]), "sh",
               reads=(r_prep,), writes=(r_sh,))
        lo = slice(0, 64)

        def fAR(h, c, a, b_):
            return (AR if h % 2 == 0 else ARo)[lo, h // 2, c, a:b_]

        def fBK(h, c, a, b_):
            return (BK if h % 2 == 0 else BKo)[lo, h // 2, c, a:b_]

        def fBKh(h, c, a, b_):
            return (BKh if h % 2 == 0 else BKho)[lo, h // 2, c, a:b_]

        def fV(h, c):
            return vpad[lo, h // 2, 64 + c * 64:128 + c * 64] if h % 2 == 0 else vo[lo, h // 2, c * 64:(c + 1) * 64]

        def fRK(h, c):
            return rkT[lo, h // 2, 64 + c * 64:128 + c * 64] if h % 2 == 0 else rko[lo, h // 2, c * 64:(c + 1) * 64]

        def fWc(h, c):
            return (Wc if h % 2 == 0 else Wco)[lo, h // 2, c:c + 1]
        RD = (r_prep, r_proj, r_sh)
        idB = identB[lo, lo]
        ot, r_ot = oTt[0], r_oTt[0]
        for c in range(NCH):
            for h in range(8):
                Sx.op("pe", lambda e, h=h: e.matmul(
                    PB[h // 4][lo, (h % 4) * 128:(h % 4 + 1) * 128], lhsT=fBK(h, c, 0, 64), rhs=fAR(h, c, 0, 128),
                    start=True, stop=True), reads=RD, writes=(PR[h // 4],))
            for h in range(8):
                Sx.op("pe", lambda e, h=h: e.matmul(
                    PB[2 + h // 4][lo, (h % 4) * 128:(h % 4 + 1) * 128], lhsT=fBK(h, c, 64, 128), rhs=fAR(h, c, 0, 128),
                    start=True, stop=True), reads=RD, writes=(PR[2 + h // 4],))
            for h in range(8):
                Sx.op("pe", lambda e, h=h: e.matmul(
                    PB[4][lo, h * 64:(h + 1) * 64], lhsT=fAR(h, c, 0, 64), rhs=fBK(h, c, 0, 64),
                    start=True, stop=True), reads=RD, writes=(PR[4],))
            for h in range(8):
                Sx.op("pe", lambda e, h=h: e.transpose(out=pbf(5)[lo, h * 64:(h + 1) * 64], in_=fBKh(h, c, 0, 64),
                                                       identity=idB), reads=RD + (r_cb,), writes=(PR[5],))
            for h in range(8):
                Sx.op("pe", lambda e, h=h: e.transpose(out=pbf(5)[lo, 512 + h * 64:512 + (h + 1) * 64],
                                                       in_=fBKh(h, c, 64, 128), identity=idB),
                      reads=RD + (r_cb,), writes=(PR[5],))
            for h in range(8):
                Sx.op("pe", lambda e, h=h: e.transpose(out=pbf(6)[lo, h * 64:(h + 1) * 64], in_=fV(h, c),
                                                       identity=idB), reads=RD + (r_cb,), writes=(PR[6],))
            Sx.op("act", lambda e: e.activation(out=BhT[:, :, :], in_=v3(pbf(5)[lo, 0:512], 8), func=AF.Copy),
                  reads=(PR[5],), writes=(r_BKhT,))
            Sx.op("act", lambda e: e.activation(out=KhT[:, :, :], in_=v3(pbf(5)[lo, 512:1024], 8), func=AF.Copy),
                  reads=(PR[5],), writes=(r_BKhT,))
            Sx.op("act", lambda e: e.activation(out=Vt[:, :, :], in_=v3(pbf(6)[lo, 0:512], 8), func=AF.Copy),
                  reads=(PR[6],), writes=(r_V,))
            if STOP == "s0":
                return finish()
            for b in range(2):
                Sx.op("dve", lambda e, b=b: e.tensor_tensor(
                    out=GTb[:, 4 * b:4 * b + 4, :], in0=v3(PB[b][lo, :], 4), in1=v3(maskGT[lo, :], 4), op=OP.mult),
                    reads=(PR[b], r_cst), writes=(r_GTm,))
                Sx.op("dve", lambda e, b=b: e.tensor_tensor(
                    out=GTk[:, 4 * b:4 * b + 4, :], in0=v3(PB[2 + b][lo, :], 4), in1=v3(maskGT[lo, :], 4), op=OP.mult),
                    reads=(PR[2 + b], r_cst), writes=(r_GTm,))
                Sx.op("dve", lambda e, b=b: e.tensor_tensor(
                    out=Xtb[0][:, 4 * b:4 * b + 4, :], in0=v3(PB[b][lo, :], 4)[:, :, 0:64],
                    in1=v3(maskGT[lo, :], 4)[:, :, 0:64], op=OP.mult),
                    reads=(PR[b], r_cst), writes=(r_Xtb[0],))
            Sx.op("dve", lambda e: e.tensor_tensor(out=Xb[0][:, :, :], in0=v3(PB[4][lo, :], 8),
                                                   in1=v3(maskL8, 8), op=OP.mult),
                  reads=(PR[4], r_cst), writes=(r_Xb[0],))
            Sx.op("dve", lambda e: e.tensor_tensor(out=Tt[:, :, :], in0=Xtb[0][:, :, :], in1=v3(ident8, 8),
                                                   op=OP.add), reads=(r_Xtb[0], r_cst), writes=(r_Tt,))
            if STOP == "s1":
                return finish()
            for k in range(5):
                a, bnx = k % 2, (k + 1) % 2
                for h in range(8):
                    Sx.op("pe", lambda e, h=h, a=a: e.matmul(
                        PB[0][lo, h * 64:(h + 1) * 64], lhsT=Xtb[a][:, h, :], rhs=Xb[a][:, h, :],
                        start=True, stop=True), reads=(r_Xtb[a], r_Xb[a]), writes=(PR[0],))
                if k < 4:
                    for h in range(8):
                        Sx.op("pe", lambda e, h=h, a=a: e.matmul(
                            PB[1][lo, h * 64:(h + 1) * 64], lhsT=Xb[a][:, h, :], rhs=Xtb[a][:, h, :],
                            start=True, stop=True), reads=(r_Xtb[a], r_Xb[a]), writes=(PR[1],))
                Sx.op("act", lambda e, bnx=bnx: e.activation(out=Xb[bnx][:, :, :], in_=v3(PB[0][lo, :], 8),
                                                             func=AF.Copy), reads=(PR[0],), writes=(r_Xb[bnx],))
                if k < 4:
                    Sx.op("dve", lambda e, bnx=bnx: e.tensor_copy(out=Xtb[bnx][:, :, :], in_=v3(PB[1][lo, :], 8)),
                          reads=(PR[1],), writes=(r_Xtb[bnx],))
                for h in range(8):
                    Sx.op("pe", lambda e, h=h, bnx=bnx: e.matmul(
                        PB[7][lo, h * 64:(h + 1) * 64], lhsT=Xb[bnx][:, h, :], rhs=Tt[:, h, :],
                        start=True, stop=True), reads=(r_Xb[bnx], r_Tt), writes=(PR[7],))
                Sx.op("dve", lambda e: e.tensor_tensor(out=Tt[:, :, :], in0=Tt[:, :, :], in1=v3(PB[7][lo, :], 8),
                                                       op=OP.add), reads=(PR[7], r_Tt), writes=(r_Tt,))
            if STOP == "s2":
                return finish()
            for h in range(8):
                Sx.op("pe", lambda e, h=h: e.matmul(
                    PB[2][lo, h * 64:(h + 1) * 64], lhsT=fAR(h, c, 0, 64), rhs=Mb[:, h, :],
                    start=True, stop=False), reads=RD + (r_M,), writes=(PR[2],))
                Sx.op("pe", lambda e, h=h: e.matmul(
                    PB[2][lo, h * 64:(h + 1) * 64], lhsT=GTk[:, h, 0:64], rhs=Vt[:, h, :],
                    start=False, stop=True), reads=(r_GTm, r_V), writes=(PR[2],))
            Sx.op("act", lambda e: e.activation(out=Psb[:, :, :], in_=v3(PB[2][lo, :], 8), func=AF.Copy),
                  reads=(PR[2],), writes=(r_Psb,))
            for h in range(8):
                Sx.op("pe", lambda e, h=h: e.matmul(
                    PB[6][lo, h * 64:(h + 1) * 64], lhsT=Tt[:, h, :], rhs=Psb[:, h, :], start=True, stop=True),
                    reads=(r_Tt, r_Psb), writes=(PR[6],))
            Sx.op("dve", lambda e: e.tensor_copy(out=Ut[:, :, :], in_=v3(PB[6][lo, :], 8)),
                  reads=(PR[6],), writes=(r_U,))
            for h in range(8):
                Sx.op("pe", lambda e, h=h: e.matmul(
                    PB[7][lo, h * 64:(h + 1) * 64], lhsT=fAR(h, c, 64, 128), rhs=Mb[:, h, :],
                    start=True, stop=False), reads=RD + (r_M,), writes=(PR[7],))
                Sx.op("pe", lambda e, h=h: e.matmul(
                    PB[7][lo, h * 64:(h + 1) * 64], lhsT=GTb[:, h, 64:128], rhs=Ut[:, h, :],
                    start=False, stop=False), reads=(r_GTm, r_U), writes=(PR[7],))
                Sx.op("pe", lambda e, h=h: e.matmul(
                    PB[7][lo, h * 64:(h + 1) * 64], lhsT=GTk[:, h, 64:128], rhs=Vt[:, h, :],
                    start=False, stop=True), reads=(r_GTm, r_V), writes=(PR[7],))
            for h in range(8):
                Sx.op("pe", lambda e, h=h: e.matmul(
                    PB[3][lo, h * 64:(h + 1) * 64], lhsT=BhT[:, h, :], rhs=Ut[:, h, :],
                    start=True, stop=False), reads=(r_BKhT, r_U), writes=(PR[3],))
                Sx.op("pe", lambda e, h=h: e.matmul(
                    PB[3][lo, h * 64:(h + 1) * 64], lhsT=KhT[:, h, :], rhs=Vt[:, h, :],
                    start=False, stop=True), reads=(r_BKhT, r_V), writes=(PR[3],))
            for h in range(8):
                Sx.op("dve", lambda e, h=h: e.scalar_tensor_tensor(
                    out=M32[:, h, :], in0=M32[:, h, :], scalar=fWc(h, c),
                    in1=PB[3][lo, h * 64:(h + 1) * 64], op0=OP.mult, op1=OP.add),
                    reads=(r_M, r_prep, r_sh, PR[3]), writes=(r_M,))
            Sx.op("act", lambda e: e.activation(out=Mb[:, :, :], in_=M32[:, :, :], func=AF.Copy),
                  reads=(r_M,), writes=(r_M,))
            if STOP == "s3":
                return finish()
            hi = lo
            Sx.op("act", lambda e: e.activation(out=ysb[hi, :, :], in_=v3(PB[7][hi, :], 8), func=AF.Copy),
                  reads=(PR[7],), writes=(r_y,))
            Sx.op("act", lambda e: e.activation(out=ysq[hi, :, :], in_=v3(PB[7][hi, :], 8), func=AF.Square),
                  reads=(PR[7],), writes=(r_y,))
            Sx.op("dve", lambda e: e.tensor_reduce(out=st[hi, 0:8], in_=ysb[hi, :, :], axis=AX.X, op=OP.add),
                  reads=(r_y,), writes=(r_st,))
            Sx.op("dve", lambda e: e.tensor_reduce(out=st[hi, 8:16], in_=ysq[hi, :, :], axis=AX.X, op=OP.add),
                  reads=(r_y,), writes=(r_st,))
            Sx.op("dve", lambda e: e.tensor_scalar(out=st[hi, 16:24], in0=st[hi, 0:8], scalar1=1.0 / HD,
                                                   scalar2=None, op0=OP.mult), reads=(r_st,), writes=(r_st,))
            Sx.op("dve", lambda e: e.tensor_tensor(out=st[hi, 24:32], in0=st[hi, 16:24], in1=st[hi, 16:24],
                                                   op=OP.mult), reads=(r_st,), writes=(r_st,))
            Sx.op("dve", lambda e: e.scalar_tensor_tensor(out=st[hi, 32:40], in0=st[hi, 8:16], scalar=1.0 / HD,
                                                          in1=st[hi, 24:32], op0=OP.mult, op1=OP.subtract),
                  reads=(r_st,), writes=(r_st,))
            Sx.op("act", lambda e: e.activation(out=st[hi, 40:48], in_=st[hi, 32:40], func=AF.Sqrt,
                                                bias=epsn[hi, 1:2], scale=1.0), reads=(r_st, r_cb), writes=(r_st,))
            Sx.op("dve", lambda e: e.reciprocal(out=st[hi, 40:48], in_=st[hi, 40:48]), reads=(r_st,), writes=(r_st,))
            for h in range(8):
                Sx.op("dve", lambda e, h=h: e.tensor_scalar(
                    out=ysb[hi, h, :], in0=ysb[hi, h, :], scalar1=st[hi, 16 + h:17 + h], scalar2=st[hi, 40 + h:41 + h],
                    op0=OP.subtract, op1=OP.mult), reads=(r_y, r_st), writes=(r_y,))
            Sx.op("dve", lambda e: e.tensor_tensor(out=ysb[hi, :, :], in0=ysb[hi, :, :], in1=v3(lnx[hi, 0:512], 8),
                                                   op=OP.mult), reads=(r_y, r_cst), writes=(r_y,))
            Sx.op("dve", lambda e: e.tensor_tensor(out=ysb[hi, :, :], in0=ysb[hi, :, :], in1=v3(lnx[hi, 512:1024], 8),
                                                   op=OP.add), reads=(r_y, r_cst), writes=(r_y,))
            for h in range(8):
                Sx.op("pe", lambda e, h=h: e.matmul(
                    PB[1][lo, 2 * h:2 * h + 2], lhsT=fRK(h, c), rhs=onesB[lo, 0:2],
                    start=True, stop=True), reads=RD + (r_cb,), writes=(PR[1],))
            Sx.op("act", lambda e: e.activation(out=st[hi, 48:56], in_=v3(PB[1][hi, 0:16], 8)[:, :, 0], func=AF.Copy),
                  reads=(PR[1],), writes=(r_st,))
            for h in range(8):
                Sx.op("dve", lambda e, h=h: e.scalar_tensor_tensor(
                    out=ysb[hi, h, :], in0=Vt[:, h, :], scalar=st[hi, 48 + h:49 + h], in1=ysb[hi, h, :],
                    op0=OP.mult, op1=OP.add), reads=(r_V, r_st, r_y), writes=(r_y,))
            for k2 in range(2):
                Sx.op("pe", lambda e, k2=k2: e.matmul(
                    PB[0][lo, 0:512], lhsT=sgd[:, k2, 64 + c * 64:128 + c * 64], rhs=wlg[:, k2 * 512:(k2 + 1) * 512],
                    start=(k2 == 0), stop=(k2 == 1)), reads=(r_proj, r_cb), writes=(PR[0],))
            Sx.op("dve", lambda e: e.tensor_tensor(out=ob[hi, :], in0=ysb[hi, :, :].rearrange("p a b -> p (a b)"),
                                                   in1=PB[0][hi, 0:512], op=OP.mult),
                  reads=(r_y, PR[0]), writes=(r_ob,))
            for cc in range(4):
                Sx.op("pe", lambda e, cc=cc: e.transpose(
                    out=pbf(4)[:, cc * 64:(cc + 1) * 64], in_=ob[hi, cc * 128:(cc + 1) * 128],
                    identity=idB), reads=(r_ob, r_cb), writes=(PR[4],))
            Sx.op("act", lambda e: e.activation(out=ot[:, :, c * 64:(c + 1) * 64], in_=v3(pbf(4)[:, 0:256], 4),
                                                func=AF.Copy), reads=(PR[4],), writes=(r_ot,))
        q, tq = (t * TT) // TQ, (t * TT) % TQ
        dst = oT_loc[q].ap().bitcast(BF).rearrange("(c p) t -> p c t", p=128)[:, :, tq:tq + TT]
        Sx.dma("sp", lambda g, dst=dst, ot=ot: g.dma_start(out=dst, in_=ot[:, :, :]), "st_o", reads=(r_ot,))
        if dbg:
            dsto = dbg_o.rearrange("(c p) t -> p c t", p=128)[:, :, t * TT:(t + 1) * TT]
            Sx.dma("sp", lambda g, dsto=dsto, ot=ot: g.dma_start(out=dsto, in_=ot[:, :, :]), "st_o", reads=(r_ot,))

    if not phase2:
        return finish()
    Sx.wait_all("pool", ["st_o"])
    for q in range(4):
        ins = nc.gpsimd.collective_compute("AllGather", OP.bypass, replica_groups=[list(range(NCORE))],
                                           ins=[oT_loc[q].ap().opt()], outs=[og[q].ap().opt()])
        ins.then_inc(Sx.sem["cc"], 1)
        Sx.cnt["cc"] += 1
    r_og = Res("og")
    r_og.w = ("cc", Sx.cnt["cc"])
    barrier()
    st1.close()
    st2 = contextlib.ExitStack()
    cur[0] = st2
    N2 = TT2
    xT = sb("xT", [128, KC, N2])
    r_xT = Res("xT")
    hT2 = sb("hT2", [128, KC, N2], BF)
    r_hT2 = Res("hT2")
    oraw = sb("oraw", [128, 8, N2], BF)
    r_oraw = Res("oraw")
    Sx.newsem("ld_og", dma=True)
    oT2 = sb("oT2", [128, 16, N2], BF)
    r_oT2 = Res("oT2")
    ybin = sb("ybin", [128, 16, N2], BF)
    r_ybin = Res("ybin")
    orow = sb("orow", [128, D])
    r_merged = Res("merged")
    r_orow = r_merged
    merged = orow[:, :].bitcast(BF).rearrange("p (k t) -> p k t", t=N2)
    actb = sb("actb", [128, 11, N2], BF)
    r_actb = Res("actb")
    zc = tpbuf[:, 2000:2032].rearrange("p (m two) -> p m two", two=2)
    fc = tpbuf[:, 2100:2100 + 2 * NFF].rearrange("p (m two) -> p m two", two=2)
    r_zc = Res("zc")
    r_fc = Res("fc")
    Sx.op("dve", lambda e: e.memset(zc[:, :, :], 0.0), writes=(r_zc,))
    Sx.op("dve", lambda e: e.memset(fc[:, :, :], 0.0), writes=(r_fc,))
    NTMP = 6
    t2 = [tpbuf[:, i * 260:i * 260 + N2 + 2] for i in range(NTMP)]
    r_t2 = [Res("t2_%d" % i) for i in range(NTMP)]
    sq2 = [sb("sq2_%d" % i, [128, N2], BF) for i in range(2)]
    r_sq2 = [Res("sq2_%d" % i) for i in range(2)]
    rstd2 = tpbuf[:, 1600:1600 + N2]
    r_rstd2 = Res("rstd2")
    Sx.newsem("st_out", dma=True)
    cwm = vec2[:, 0:48]
    cwf = vec2[:, 48:48 + 3 * NFF]
    selv = vec2[:, 48 + 3 * NFF:48 + 3 * NFF + 8]
    selh = vec2[:, 48 + 3 * NFF + 8:48 + 3 * NFF + 16]
    hv = vec2[:, 48 + 3 * NFF + 16:48 + 3 * NFF + 17]

    def acc(bank, nk, rhs_fn, nt, rreads, Mw=128):
        wt, wr = WS.next()
        wt3 = wt.rearrange("p (k c) -> p k c", c=128)
        for kc in range(nk):
            Sx.op("pe", lambda e, kc=kc: e.matmul(PB[bank][0:Mw, 0:nt], lhsT=wt3[:, kc, 0:Mw], rhs=rhs_fn(kc),
                                                  start=(kc == 0), stop=(kc == nk - 1)),
                  reads=(wr,) + tuple(rreads), writes=(PR[bank],))

    def rms_bcast(nt, r_dst):
        for n in range(KC):
            q_, rq = sq2[n % 2], r_sq2[n % 2]
            Sx.op("act", lambda e, n=n, q_=q_: e.activation(out=q_[:, 0:nt], in_=xT[:, n, 0:nt], func=AF.Square),
                  reads=(r_xT,), writes=(rq,))
            Sx.op("pe", lambda e, n=n, q_=q_: e.matmul(PB[2][:, 0:nt], lhsT=onesB, rhs=q_[:, 0:nt],
                                                       start=(n == 0), stop=(n == KC - 1)),
                  reads=(rq, r_cb), writes=(PR[2],))
        Sx.op("act", lambda e: e.activation(out=rstd2[:, 0:nt], in_=PB[2][:, 0:nt], func=AF.Sqrt,
                                            bias=epsn[:, 0:1], scale=1.0 / D), reads=(PR[2], r_cb), writes=(r_dst,))
        Sx.op("dve", lambda e: e.reciprocal(out=rstd2[:, 0:nt], in_=rstd2[:, 0:nt]), reads=(r_dst,), writes=(r_dst,))

    def conv3(dst_fn, buf, nt, w_ap, widx, stride, rbuf, rd, wr_):
        ta, tb = t2[4], t2[5]
        Sx.op("dve", lambda e: e.tensor_scalar(out=ta[:, 0:nt], in0=buf[:, 0:nt], scalar1=w_ap[:, widx:widx + 1],
                                               scalar2=None, op0=OP.mult), reads=(rbuf, r_cst), writes=(r_t2[4],))
        Sx.op("dve", lambda e: e.scalar_tensor_tensor(out=tb[:, 0:nt], in0=buf[:, 1:nt + 1],
                                                      scalar=w_ap[:, widx + stride:widx + stride + 1], in1=ta[:, 0:nt],
                                                      op0=OP.mult, op1=OP.add),
              reads=(rbuf, r_t2[4], r_cst), writes=(r_t2[5],))
        Sx.op("dve", lambda e: e.scalar_tensor_tensor(out=dst_fn, in0=buf[:, 2:nt + 2],
                                                      scalar=w_ap[:, widx + 2 * stride:widx + 2 * stride + 1],
                                                      in1=tb[:, 0:nt], op0=OP.mult, op1=OP.add),
              reads=(rbuf, r_t2[5], r_cst) + tuple(rd), writes=tuple(wr_))

    def p2_tile(row0, nt, halo, col0, selvec):
        nsub = (nt + 127) // 128
        for s_ in range(nsub):
            rows = min(128, nt - s_ * 128)
            norm_sub(x2_d[row0 + s_ * 128:row0 + s_ * 128 + rows, :], rows, xT[:, :, s_ * 128:s_ * 128 + rows], r_xT,
                     hT2[:, :, s_ * 128:s_ * 128 + rows], r_hT2, g1, sh1)
        for k4 in range(16):
            for q8 in range(8):
                q, bb = q8 // 2, q8 % 2
                src = og[q].ap().bitcast(BF).rearrange("(k p) t -> p k t", p=128)[:, bb * 16 + k4, col0:col0 + nt]
                Sx.dma("sp", lambda g, q8=q8, src=src: g.dma_start(out=oraw[:, q8, 0:nt], in_=src), "ld_og",
                       reads=(r_og,), writes=(r_oraw,))
            dst = oT2[:, k4, 0:nt]
            Sx.op("dve", lambda e, dst=dst: e.tensor_scalar(out=dst, in0=oraw[:, 0, 0:nt], scalar1=selvec[:, 0:1],
                                                            scalar2=None, op0=OP.mult),
                  reads=(r_oraw, r_cst), writes=(r_oT2,))
            for q8 in range(1, 8):
                Sx.op("dve", lambda e, dst=dst, q8=q8: e.scalar_tensor_tensor(
                    out=dst, in0=oraw[:, q8, 0:nt], scalar=selvec[:, q8:q8 + 1], in1=dst, op0=OP.mult, op1=OP.add),
                    reads=(r_oraw, r_oT2, r_cst), writes=(r_oT2,))
        for m in range(16):
            for k in range(3):
                acc(3 + k, KC, lambda kc: hT2[:, kc, 0:nt], nt, (r_hT2,))
            zb, rzb = t2[0], r_t2[0]
            Sx.op("act", lambda e: e.activation(out=t2[1][:, 0:nt], in_=PB[4][:, 0:nt], func=AF.Copy),
                  reads=(PR[4],), writes=(r_t2[1],))
            Sx.op("dve", lambda e, m=m: e.tensor_copy(out=zb[:, 0:2], in_=zc[:, m, :]), reads=(r_zc,), writes=(rzb,))
            Sx.op("dve", lambda e: e.tensor_tensor(out=zb[:, 2:nt + 2], in0=t2[1][:, 0:nt], in1=PB[5][:, 0:nt],
                                                   op=OP.mult), reads=(r_t2[1], PR[5]), writes=(rzb,))
            if halo:
                Sx.op("dve", lambda e, m=m: e.tensor_scalar(out=zc[:, m, :], in0=zb[:, nt:nt + 2], scalar1=hv,
                                                            scalar2=None, op0=OP.mult),
                      reads=(rzb, r_cst), writes=(r_zc,))
            else:
                Sx.op("dve", lambda e, m=m: e.tensor_copy(out=zc[:, m, :], in_=zb[:, nt:nt + 2]),
                      reads=(rzb,), writes=(r_zc,))
            conv3(t2[2][:, 0:nt], zb, nt, cwm, m, 16, rzb, (), (r_t2[2],))
            Sx.op("dve", lambda e, m=m: e.tensor_tensor(out=ybin[:, m, 0:nt], in0=t2[2][:, 0:nt], in1=PB[3][:, 0:nt],
                                                        op=OP.mult), reads=(r_t2[2], PR[3]), writes=(r_ybin,))
        for n in range(KC):
            acc(3, KC, lambda kc: hT2[:, kc, 0:nt], nt, (r_hT2,))
            acc(4, KC, lambda kc: hT2[:, kc, 0:nt], nt, (r_hT2,))
            acc(5, 16, lambda kc: oT2[:, kc, 0:nt], nt, (r_oT2,))
            acc(6, 16, lambda kc: ybin[:, kc, 0:nt], nt, (r_ybin,))
            Sx.op("act", lambda e: e.activation(out=t2[0][:, 0:nt], in_=PB[3][:, 0:nt], func=AF.Sigmoid),
                  reads=(PR[3],), writes=(r_t2[0],))
            Sx.op("act", lambda e: e.activation(out=t2[1][:, 0:nt], in_=PB[4][:, 0:nt], func=AF.Sigmoid),
                  reads=(PR[4],), writes=(r_t2[1],))
            Sx.op("dve", lambda e: e.tensor_tensor(out=t2[0][:, 0:nt], in0=t2[0][:, 0:nt], in1=PB[5][:, 0:nt],
                                                   op=OP.mult), reads=(r_t2[0], PR[5]), writes=(r_t2[0],))
            Sx.op("dve", lambda e: e.tensor_tensor(out=t2[1][:, 0:nt], in0=t2[1][:, 0:nt], in1=PB[6][:, 0:nt],
                                                   op=OP.mult), reads=(r_t2[1], PR[6]), writes=(r_t2[1],))
            Sx.op("dve", lambda e, n=n: e.tensor_tensor(out=merged[:, n, 0:nt], in0=t2[0][:, 0:nt], in1=t2[1][:, 0:nt],
                                                        op=OP.add), reads=(r_t2[0], r_t2[1]), writes=(r_merged,))
        for n in range(KC):
            b = 3 + n % 2
            acc(b, KC, lambda kc: merged[:, kc, 0:nt], nt, (r_merged,))
            Sx.op("dve", lambda e, n=n, b=b: e.scalar_tensor_tensor(
                out=xT[:, n, 0:nt], in0=PB[b][:, 0:nt], scalar=gate1[:, n:n + 1], in1=xT[:, n, 0:nt],
                op0=OP.mult, op1=OP.add), reads=(PR[b], r_xT, r_mod), writes=(r_xT,))
        rms_bcast(nt, r_rstd2)
        for n in range(KC):
            ti, tr = t2[n % 2], r_t2[n % 2]
            Sx.op("dve", lambda e, n=n, ti=ti: e.scalar_tensor_tensor(
                out=ti[:, 0:nt], in0=xT[:, n, 0:nt], scalar=g2[:, n:n + 1], in1=rstd2[:, 0:nt],
                op0=OP.mult, op1=OP.mult), reads=(r_xT, r_rstd2, r_mod), writes=(tr,))
            Sx.op("act", lambda e, n=n, ti=ti: e.activation(out=hT2[:, n, 0:nt], in_=ti[:, 0:nt], func=AF.Identity,
                                                            bias=sh2[:, n:n + 1], scale=1.0),
                  reads=(tr, r_mod), writes=(r_hT2,))
        m0 = 0
        for gsz in FFG:
            for mi in range(gsz):
                m = m0 + mi
                acc(3, KC, lambda kc: hT2[:, kc, 0:nt], nt, (r_hT2,))
                if not halo:
                    acc(4, KC, lambda kc: hT2[:, kc, 0:nt], nt, (r_hT2,))
                ug, rug = t2[3], r_t2[3]
                Sx.op("act", lambda e: e.activation(out=ug[:, 2:nt + 2], in_=PB[3][:, 0:nt], func=AF.Copy),
                      reads=(PR[3],), writes=(rug,))
                Sx.op("dve", lambda e, m=m: e.tensor_copy(out=ug[:, 0:2], in_=fc[:, m, :]), reads=(r_fc,), writes=(rug,))
                if halo:
                    Sx.op("dve", lambda e, m=m: e.tensor_scalar(out=fc[:, m, :], in0=ug[:, nt:nt + 2], scalar1=hv,
                                                                scalar2=None, op0=OP.mult),
                          reads=(rug, r_cst), writes=(r_fc,))
                    continue
                Sx.op("dve", lambda e, m=m: e.tensor_copy(out=fc[:, m, :], in_=ug[:, nt:nt + 2]),
                      reads=(rug,), writes=(r_fc,))
                conv3(t2[2][:, 0:nt], ug, nt, cwf, m, NFF, rug, (), (r_t2[2],))
                Sx.op("act", lambda e: e.activation(out=t2[1][:, 0:nt], in_=t2[2][:, 0:nt], func=AF.Silu),
                      reads=(r_t2[2],), writes=(r_t2[1],))
                Sx.op("dve", lambda e, mi=mi: e.tensor_tensor(out=actb[:, mi, 0:nt], in0=t2[1][:, 0:nt],
                                                              in1=PB[4][:, 0:nt], op=OP.mult),
                      reads=(r_t2[1], PR[4]), writes=(r_actb,))
            if not halo:
                for nn in range(KC):
                    b = 5 + nn % 2
                    acc(b, gsz, lambda kc: actb[:, kc, 0:nt], nt, (r_actb,))
                    Sx.op("dve", lambda e, nn=nn, b=b: e.scalar_tensor_tensor(
                        out=xT[:, nn, 0:nt], in0=PB[b][:, 0:nt], scalar=gate2[:, nn:nn + 1], in1=xT[:, nn, 0:nt],
                        op0=OP.mult, op1=OP.add), reads=(PR[b], r_xT, r_mod), writes=(r_xT,))
            m0 += gsz
        if halo:
            return
        rms_bcast(nt, r_rstd2)
        for n in range(KC):
            Sx.op("dve", lambda e, n=n: e.scalar_tensor_tensor(
                out=xT[:, n, 0:nt], in0=xT[:, n, 0:nt], scalar=fgain[:, n:n + 1], in1=rstd2[:, 0:nt],
                op0=OP.mult, op1=OP.mult), reads=(r_xT, r_rstd2, r_cst), writes=(r_xT,))
        for s_ in range(nsub):
            for grp in range(8):
                b = grp % 2
                for k4 in range(4):
                    n = grp * 4 + k4
                    Sx.op("pe", lambda e, n=n, k4=k4, b=b: e.transpose(
                        out=PB[b][:, k4 * 128:(k4 + 1) * 128], in_=xT[:, n, s_ * 128:(s_ + 1) * 128],
                        identity=identF), reads=(r_xT, r_cst), writes=(PR[b],))
                eng = "act" if grp % 2 == 0 else "dve"
                if eng == "act":
                    Sx.op("act", lambda e, grp=grp, b=b: e.activation(out=orow[:, grp * 512:(grp + 1) * 512],
                                                                      in_=PB[b][:, :], func=AF.Copy),
                          reads=(PR[b],), writes=(r_orow,))
                else:
                    Sx.op("dve", lambda e, grp=grp, b=b: e.tensor_copy(out=orow[:, grp * 512:(grp + 1) * 512],
                                                                       in_=PB[b][:, :]),
                          reads=(PR[b],), writes=(r_orow,))
            r_out0 = row0 - 4 + s_ * 128
            Sx.dma("sp", lambda g, r_out0=r_out0: g.dma_start(out=out_d[r_out0:r_out0 + 128, :], in_=orow[:, :]),
                   "st_out", reads=(r_orow,))

    p2_tile(0, 4, True, TQ - 4, selh)
    for t in range(NT2):
        p2_tile(4 + t * TT2, TT2, False, t * TT2, selv)
    Sx.wait_all("sp", ["st_out"])
    Sx.wait_all("pool", ["st_out"])
    return finish()


def _unused():
    Sx.wait_all("sp", ["st_o"])
    Sx.wait_all("pool", ["pe", "act", "dve", "st_o"])
    return nc, stack


def _tile_w(w, ncols_pad=None):
    K, N = w.shape
    kc = K // 128
    return np.ascontiguousarray(w.reshape(kc, 128, N // 128, 128).transpose(2, 1, 0, 3).reshape(N // 128, 128, kc * 128))


def _vecT(v):
    return np.ascontiguousarray(v.reshape(-1, 128).T)


def _consts(TT):
    c = np.zeros((128, 2176), np.float32)
    c[:, 0:128] = np.eye(128, dtype=np.float32)
    s = np.arange(128) % 64
    j = np.arange(128)
    m = np.where(j[None, :] < 64, s[:, None] < j[None, :], s[:, None] <= (j[None, :] - 64)).astype(np.float32)
    c[:, 128:640] = np.tile(m, (1, 4))
    tl = (np.arange(64)[None, :] < np.arange(64)[:, None]).astype(np.float32)
    c[0:64, 640:1152] = np.tile(tl, (1, 8))
    c[0:64, 1152:1664] = np.tile(np.eye(64, dtype=np.float32), (1, 8))
    cm = np.ones(512, np.float32)
    cm[::64] = 0.0
    c[:, 1664:2176] = cm[None, :]
    return c


def prep_inputs(inp, S, phase2=True):
    f = lambda a: np.asarray(a, dtype=np.float32)
    TQ = S // 4
    TT = min(512, TQ)
    x = f(inp["x"])[:, :S]
    w_in = f(inp["w_in"])[0]
    NSH = 3 * DR + 96 + 96 + 256
    mu = f(inp["mu_shift"])[0]
    shared = {}
    shared["wada"] = _tile_w(f(inp["w_ada"])[0])
    if phase2:
        wc_cols = w_in[:, NSH:NSH + 3 * DR]
        wg_cols = w_in[:, NSH + 3 * DR:]
        t_cb, t_cc, t_cx = (_tile_w(wc_cols[:, i * DR:(i + 1) * DR]) for i in range(3))
        shared["w2c"] = np.ascontiguousarray(np.stack([t_cb, t_cc, t_cx], axis=1).reshape(48, 128, KC * 128))
        t_ga, t_gb = _tile_w(wg_cols[:, :D]), _tile_w(wg_cols[:, D:])
        shared["w2g"] = np.ascontiguousarray(np.stack([t_ga, t_gb], axis=1).reshape(64, 128, KC * 128))
        shared["wor"] = _tile_w(f(inp["w_o_rwkv"])[0])
        shared["woc"] = _tile_w(f(inp["w_o_conv"])[0])
        shared["wout"] = _tile_w(f(inp["w_out"])[0])
        shared["wup"] = _tile_w(f(inp["w_ffn_up"])[0])
        shared["wdn"] = _tile_w(f(inp["w_ffn_down"])[0])
    cst = _consts(TT)
    shared["cst"] = cst
    gains = np.concatenate([_vecT(f(inp["norm1_gain"])[0]), _vecT(f(inp["norm2_gain"])[0]),
                            _vecT(f(inp["final_gain"]))], axis=1)
    shared["vec0"] = np.ascontiguousarray(np.concatenate([_vecT(f(inp["b_ada"])[0]), gains], axis=1))
    cwm = f(inp["conv_w_mix"])[0]
    cwf = f(inp["conv_w_ffn"])[0]
    maps = []
    for i in range(NCORE):
        b, j = i // 4, i % 4
        m = dict(shared)
        m["x"] = np.ascontiguousarray(x[b])
        x2 = np.zeros((4 + TQ, D), np.float32)
        x2[4:] = x[b, j * TQ:(j + 1) * TQ]
        if j > 0:
            x2[:4] = x[b, j * TQ - 4:j * TQ]
        if phase2:
            m["x2"] = x2
        m["cT"] = _vecT(f(inp["c"])[b])
        ch = slice(j * 512, (j + 1) * 512)
        cols = np.concatenate([np.arange(k * DR + j * 512, k * DR + (j + 1) * 512) for k in range(3)])
        w1 = np.zeros((D, 16 * 128), np.float32)
        w1[:, 0:1536] = w_in[:, cols]
        w1[:, 1536:1632] = w_in[:, 3 * DR:3 * DR + 96]
        w1[:, 1664:1760] = w_in[:, 3 * DR + 96:3 * DR + 192]
        w1[:, 1792:2048] = w_in[:, 3 * DR + 192:3 * DR + 448]
        m["w1"] = _tile_w(w1)
        mu1 = np.zeros(16 * 128, np.float32)
        mu1[0:1536] = mu[cols]
        mu1[1536:1632] = mu[3 * DR:3 * DR + 96]
        mu1[1664:1760] = mu[3 * DR + 96:3 * DR + 192]
        mu1[1792:2048] = mu[3 * DR + 192:3 * DR + 448]
        v1 = [_vecT(mu1)]
        for nm in ("w0", "a0", "k_k", "k_a"):
            v1.append(_vecT(f(inp[nm])[0][ch]))
        v1.append(_vecT(f(inp["r_k"])[0].reshape(-1)[ch]))
        m["vec1"] = np.ascontiguousarray(np.concatenate(v1, axis=1))
        m["lnx"] = np.ascontiguousarray(np.tile(np.concatenate([f(inp["lnx_w"])[0][ch], f(inp["lnx_b"])[0][ch]])[None, :],
                                                (128, 1)))
        m["wld"] = np.ascontiguousarray(f(inp["w_lora_decay"])[0][:, ch])
        m["wli"] = np.ascontiguousarray(f(inp["w_lora_iclr"])[0][:, ch])
        wg = f(inp["w_lora_gate"])[0][:, ch]
        m["wlg"] = np.ascontiguousarray(wg.reshape(2, 128, 512).transpose(1, 0, 2).reshape(128, 1024))
        v2 = np.zeros((128, 48 + 3 * NFF + 32), np.float32)
        for k in range(3):
            v2[:, 16 * k:16 * k + 16] = _vecT(cwm[k])
            v2[:, 48 + NFF * k:48 + NFF * (k + 1)] = _vecT(cwf[k])
        o = 48 + 3 * NFF
        v2[:, o + 2 * j + b] = 1.0
        if j > 0:
            v2[:, o + 8 + 2 * (j - 1) + b] = 1.0
            v2[:, o + 16] = 1.0
        m["vec2"] = v2
        maps.append(m)
    return maps


_CACHE = {}


def kernel(**inputs):
    S = 8192
    if S not in _CACHE:
        _CACHE[S] = build(S)
    nc, _ = _CACHE[S]
    maps = prep_inputs(inputs, S)
    res = run_bass_kernel_spmd(nc, maps, core_ids=list(range(NCORE)))
    out = np.zeros((2, S, D), np.float32)
    TQ = S // 4
    for i in range(NCORE):
        out[i // 4, (i % 4) * TQ:(i % 4 + 1) * TQ] = res.results[i]["out"]
    return out
```

```python
import contextlib
import numpy as np
import ml_dtypes
import concourse.bass as bass
import concourse.mybir as mybir
from concourse.bass_utils import run_bass_kernel_spmd

F32 = mybir.dt.float32
BF = mybir.dt.bfloat16
AF = mybir.ActivationFunctionType
OP = mybir.AluOpType
AX = mybir.AxisListType

D = 4096
KC = 32
HD = 64
DR = 2048
DFF = 11008
NFF = 86
NCORE = 8
C0 = float(np.exp(-0.5))
NORM_EPS = 1e-6
LNX_EPS = 64e-5
FFG = [11, 11, 11, 11, 11, 11, 10, 10]


class Res:
    __slots__ = ("w", "r", "name")

    def __init__(self, name=""):
        self.w = None
        self.r = {}
        self.name = name


class Sched:
    def __init__(self, nc, stack):
        self.nc = nc
        self.stack = stack
        self.eng = {"pe": nc.tensor, "act": nc.scalar, "dve": nc.vector, "pool": nc.gpsimd,
                    "sp": nc.sync}
        self.sem = {}
        self.cnt = {}
        self.isdma = set()
        self.known = {e: {} for e in self.eng}
        for e in ("pe", "act", "dve", "pool"):
            self.newsem(e)

    def newsem(self, key, dma=False):
        self.sem[key] = self.stack.enter_context(self.nc.semaphore("s_" + key))
        self.cnt[key] = 0
        if dma:
            self.isdma.add(key)

    def _waits(self, e, reads, writes):
        need = {}
        for R in reads:
            if R.w is not None:
                need[R.w[0]] = max(need.get(R.w[0], 0), R.w[1])
        for W in writes:
            if W.w is not None:
                need[W.w[0]] = max(need.get(W.w[0], 0), W.w[1])
            for k, v in W.r.items():
                need[k] = max(need.get(k, 0), v)
        for k, v in need.items():
            if k in self.isdma:
                v = self.cnt[k]
            if k == e and e == "pe":
                continue
            if self.known[e].get(k, 0) < v:
                self.eng[e].wait_ge(self.sem[k], v)
                self.known[e][k] = v

    def _mark(self, key, val, reads, writes):
        for W in writes:
            W.w = (key, val)
            W.r = {}
        for R in reads:
            if R.r.get(key, 0) < val:
                R.r[key] = val

    def op(self, e, fn, reads=(), writes=()):
        self._waits(e, reads, writes)
        ins = fn(self.eng[e])
        self.cnt[e] += 1
        ins.then_inc(self.sem[e], 1)
        self._mark(e, self.cnt[e], reads, writes)

    def dma(self, q, fn, semkey, reads=(), writes=()):
        self._waits(q, reads, writes)
        ins = fn(self.eng[q])
        self.cnt[semkey] += 16
        ins.then_inc(self.sem[semkey], 16)
        self._mark(semkey, self.cnt[semkey], reads, writes)

    def wait_all(self, e, keys):
        for k in keys:
            v = self.cnt[k]
            if v and self.known[e].get(k, 0) < v:
                self.eng[e].wait_ge(self.sem[k], v)
                self.known[e][k] = v


class WStream:
    def __init__(self, S, nc, stack, nslots, slot_elems):
        self.S = S
        self.n = nslots
        self.buf = stack.enter_context(nc.sbuf_tensor("wring", [128, nslots, slot_elems], BF))
        self.res = [Res("w%d" % i) for i in range(nslots)]
        self.plan = []
        self.issued = 0
        self.used = 0
        for i in range(nslots):
            S.newsem("wd%d" % i, dma=True)

    def add(self, ap, E, dep=None):
        self.plan.append((ap, E, dep))

    def _issue(self):
        i = self.issued
        ap, E, dep = self.plan[i]
        s = i % self.n
        dst = self.buf[:, s, 0:E]
        if E > 2048 and E % 2048 == 0:
            src = ap.rearrange("p (a b) -> p a b", b=2048)
            dst = dst.rearrange("p (a b) -> p a b", b=2048)
        else:
            src = ap
        self.S.dma("pool", lambda g: g.dma_start(out=dst, in_=src), "wd%d" % s,
                   reads=(() if dep is None else (dep,)), writes=(self.res[s],))
        self.issued += 1

    def next(self):
        i = self.used
        while self.issued < min(i + self.n, len(self.plan)):
            self._issue()
        self.used += 1
        s = i % self.n
        return self.buf[:, s, :], self.res[s]


STOP = None


def build(S, dbg=False, phase2=True):
    TQ = S // 4
    TT2 = min(256, TQ)
    TT = min(256, TQ)
    NT1 = S // TT
    NT2 = TQ // TT2
    NSUB = TT // 128
    NCH = TT // 64
    nc = bass.Bass("TRN2", target_bir_lowering=False)
    stack = contextlib.ExitStack()
    Sx = Sched(nc, stack)

    def din(name, shape, dt=F32):
        return nc.dram_tensor(name, list(shape), dt, kind="ExternalInput").ap()

    x_d = din("x", [S, D])
    cT_d = din("cT", [128, KC])
    wada_d = din("wada", [192, 128, KC * 128])
    vec0_d = din("vec0", [128, 192 + 96])
    w1_d = din("w1", [16, 128, KC * 128])
    vec1_d = din("vec1", [128, 36])
    lnx_d = din("lnx", [128, 1024])
    wld_d = din("wld", [96, 512])
    wli_d = din("wli", [96, 512])
    wlg_d = din("wlg", [128, 2 * 512])
    cst_d = din("cst", [128, 2176])
    vec2_d = din("vec2", [128, 48 + 3 * NFF + 32])
    if phase2:
        x2_d = din("x2", [4 + TQ, D])
        w2c_d = din("w2c", [48, 128, KC * 128])
        w2g_d = din("w2g", [64, 128, KC * 128])
        wor_d = din("wor", [32, 128, 16 * 128])
        woc_d = din("woc", [32, 128, 16 * 128])
        wout_d = din("wout", [32, 128, KC * 128])
        wup_d = din("wup", [172, 128, KC * 128])
        wdn_d = din("wdn", [32, 128, NFF * 128])
    out_d = nc.dram_tensor("out", [TQ, D], F32, kind="ExternalOutput").ap()
    oT_loc = [nc.dram_tensor("oTloc%d" % q, [512, TQ // 2], F32) for q in range(4)]
    og = [nc.dram_tensor("og%d" % q, [2 * DR, TQ // 2], F32) for q in range(4)]
    if dbg:
        dbg_o = nc.dram_tensor("dbg_o", [512, S], BF, kind="ExternalOutput").ap()
        dbg_mod = nc.dram_tensor("dbg_mod", [128, 192], F32, kind="ExternalOutput").ap()

    cur = [stack]

    def sb(name, shape, dt=F32):
        return cur[0].enter_context(nc.sbuf_tensor("sb_" + name, list(shape), dt))

    def finish():
        Sx.wait_all("sp", ["st_o"])
        Sx.wait_all("pool", ["pe", "act", "dve", "st_o"])
        return nc, stack

    def barrier():
        keys = list(Sx.sem.keys())
        for e in ("pe", "act", "dve", "pool", "sp"):
            Sx.wait_all(e, keys)

    PB = [stack.enter_context(nc.psum_tensor("pb%d" % i, [128, 512], F32)) for i in range(8)]
    PR = [Res("pb%d" % i) for i in range(8)]

    def pbf(i):
        return PB[i][:, :].bitcast(BF)

    for k in ("ld_c", "ld_x0", "ld_x1", "st_o", "ld_o", "cc"):
        Sx.newsem(k, dma=True)
    cst = sb("cst", [128, 2176])
    r_cst = Res("cst")
    vec0 = sb("vec0", [128, 288])
    vec1 = sb("vec1", [128, 36])
    vec2 = sb("vec2", [128, 48 + 3 * NFF + 32])
    lnx = sb("lnx", [128, 1024])
    cT = sb("cT", [128, KC])
    big1 = sb("big1", [128, 4096])
    wld32 = big1[0:96, 0:512]
    wli32 = big1[0:96, 512:1024]
    wlg32 = big1[:, 1024:2048]
    for dst, src in ((cst[:, :], cst_d), (vec0[:, :], vec0_d), (vec1[:, :], vec1_d), (vec2[:, :], vec2_d), (lnx[:, :], lnx_d),
                     (cT[:, :], cT_d), (wld32, wld_d), (wli32, wli_d), (wlg32, wlg_d)):
        Sx.dma("sp", lambda g, dst=dst, src=src: g.dma_start(out=dst, in_=src), "ld_c",
               writes=(r_cst,))
    identF = cst[:, 0:128]
    maskGT = cst[:, 128:640]
    maskL8 = cst[0:64, 640:1152]
    ident8 = cst[0:64, 1152:1664]
    cmask = cst[:, 1664:1664 + TT]
    cb = sb("cb", [128, 512], BF)
    r_cb = Res("cb")
    Sx.op("dve", lambda e: e.tensor_copy(out=cb[:, 0:128], in_=identF), reads=(r_cst,), writes=(r_cb,))
    Sx.op("dve", lambda e: e.memset(cb[:, 128:256], 1.0), writes=(r_cb,))
    Sx.op("dve", lambda e: e.memset(cb[:, 256:386], 0.0), writes=(r_cb,))
    Sx.op("dve", lambda e: e.memset(cb[0:64, 256:320], 1.0), writes=(r_cb,))
    Sx.op("dve", lambda e: e.memset(cb[64:128, 320:384], 1.0), writes=(r_cb,))
    Sx.op("dve", lambda e: e.memset(cb[0:64, 384:385], 1.0), writes=(r_cb,))
    Sx.op("dve", lambda e: e.memset(cb[64:128, 385:386], 1.0), writes=(r_cb,))
    identB = cb[:, 0:128]
    onesB = cb[:, 128:256]
    blkones = cb[:, 256:384]
    hind = cb[:, 384:386]
    wld = sb("wld", [96, 512], BF)
    wli = sb("wli", [96, 512], BF)
    wlg = sb("wlg", [128, 1024], BF)
    Sx.op("dve", lambda e: e.tensor_copy(out=wld[:, :], in_=wld32), reads=(r_cst,), writes=(r_cb,))
    Sx.op("dve", lambda e: e.tensor_copy(out=wli[:, :], in_=wli32), reads=(r_cst,), writes=(r_cb,))
    Sx.op("dve", lambda e: e.tensor_copy(out=wlg[:, :], in_=wlg32), reads=(r_cst,), writes=(r_cb,))
    epsn = sb("epsn", [128, 2])
    Sx.op("dve", lambda e: e.memset(epsn[:, 0:1], NORM_EPS), writes=(r_cb,))
    Sx.op("dve", lambda e: e.memset(epsn[:, 1:2], LNX_EPS), writes=(r_cb,))

    WS = WStream(Sx, nc, stack, 4, KC * 128)
    for n in range(192):
        WS.add(wada_d[n], KC * 128)
    for t in range(NT1):
        for c in range(16):
            WS.add(w1_d[c], KC * 128)

    precast = []
    r_pre = Res("pre")
    if phase2:
        Sx.newsem("pre", dma=True)

        def mk_b(name, src_d, ntile, E):
            dstt = nc.dram_tensor(name + "_b", [ntile, 128, E], BF)
            for i in range(ntile):
                precast.append((src_d[i], dstt.ap()[i], E))
            return dstt.ap()
        w2c_b = mk_b("w2c", w2c_d, 48, KC * 128)
        w2g_b = mk_b("w2g", w2g_d, 64, KC * 128)
        wor_b = mk_b("wor", wor_d, 32, 16 * 128)
        woc_b = mk_b("woc", woc_d, 32, 16 * 128)
        wout_b = mk_b("wout", wout_d, 32, KC * 128)
        wup_b = mk_b("wup", wup_d, 172, KC * 128)
        wdn_b = mk_b("wdn", wdn_d, 32, NFF * 128)
    pre_i = [0]

    def issue_precast(n):
        for _ in range(n):
            if pre_i[0] >= len(precast):
                return
            src, dst, E = precast[pre_i[0]]
            pre_i[0] += 1
            bsz = 2048 if E % 2048 == 0 else 1376
            src3 = src.rearrange("p (a b) -> p a b", b=bsz)
            dst3 = dst.rearrange("p (a b) -> p a b", b=bsz)
            Sx.dma("pool", lambda g, src3=src3, dst3=dst3: g.dma_start(out=dst3, in_=src3), "pre", writes=(r_pre,))

    def p2_plan(halo):
        for m in range(16):
            for k in range(3):
                WS.add(w2c_b[3 * m + k], KC * 128, r_pre)
        for n in range(32):
            WS.add(w2g_b[2 * n], KC * 128, r_pre)
            WS.add(w2g_b[2 * n + 1], KC * 128, r_pre)
            WS.add(wor_b[n], 16 * 128, r_pre)
            WS.add(woc_b[n], 16 * 128, r_pre)
        for n in range(32):
            WS.add(wout_b[n], KC * 128, r_pre)
        m0 = 0
        for gsz in FFG:
            for m in range(m0, m0 + gsz):
                WS.add(wup_b[m], KC * 128, r_pre)
                if not halo:
                    WS.add(wup_b[NFF + m], KC * 128, r_pre)
            if not halo:
                for nn in range(32):
                    WS.add(wdn_b[nn][:, m0 * 128:(m0 + gsz) * 128], gsz * 128, r_pre)
            m0 += gsz
    if phase2:
        p2_plan(True)
        for t in range(NT2):
            p2_plan(False)

    cact = sb("cact", [128, KC, 2], BF)
    r_cact = Res("cact")
    for j2 in range(2):
        Sx.op("act", lambda e, j2=j2: e.activation(out=cact[:, :, j2], in_=cT[:, :], func=AF.Silu),
              reads=(r_cst,), writes=(r_cact,))
    for n in range(192):
        wt, wr = WS.next()
        wt3 = wt.rearrange("p (k c) -> p k c", c=128)
        for kc in range(KC):
            Sx.op("pe", lambda e, n=n, kc=kc, wt3=wt3: e.matmul(
                PB[0][:, 2 * n:2 * n + 2], lhsT=wt3[:, kc, :], rhs=cact[:, kc, :],
                start=(kc == 0), stop=(kc == KC - 1)), reads=(wr, r_cact), writes=(PR[0],))
    modT = sb("modT", [128, 192])
    r_mod = Res("mod")
    Sx.op("dve", lambda e: e.tensor_tensor(
        out=modT[:, :], in0=PB[0][:, 0:384].rearrange("p (n two) -> p n two", two=2)[:, :, 0],
        in1=vec0[:, 0:192], op=OP.add), reads=(PR[0], r_cst), writes=(r_mod,))
    gsh = sb("gsh", [128, 64])
    for (o, sc, gn) in ((0, 32, 192), (32, 128, 224)):
        Sx.op("dve", lambda e, o=o, sc=sc: e.tensor_scalar(
            out=gsh[:, o:o + 32], in0=modT[:, sc:sc + 32], scalar1=1.0, scalar2=None, op0=OP.add),
            reads=(r_mod,), writes=(r_mod,))
        Sx.op("dve", lambda e, o=o, gn=gn: e.tensor_tensor(
            out=gsh[:, o:o + 32], in0=gsh[:, o:o + 32], in1=vec0[:, gn:gn + 32], op=OP.mult),
            reads=(r_mod, r_cst), writes=(r_mod,))
    g1, sh1, gate1 = gsh[:, 0:32], modT[:, 0:32], modT[:, 64:96]
    g2, sh2, gate2 = gsh[:, 32:64], modT[:, 96:128], modT[:, 160:192]
    fgain = vec0[:, 256:288]
    if dbg:
        Sx.dma("sp", lambda g: g.dma_start(out=dbg_mod, in_=modT[:, :]), "st_o", reads=(r_mod,))

    if STOP == "phase0":
        return finish()
    xs_buf = [sb("xs%d" % i, [128, D // 2]) for i in range(2)]
    xs_res = [Res("xs%d" % i) for i in range(2)]
    tpbuf = sb("tpbuf", [128, 10 * 512])
    r_sq = Res("tpA")
    sq = sb("sq", [128, KC, 128], BF)
    rstd = sb("rstd", [128, 128])
    r_rstd = Res("rstd")
    ntmp = [sb("ntmp%d" % i, [128, 128]) for i in range(2)]
    r_ntmp = [Res("ntmp%d" % i) for i in range(2)]
    norm_ctr = [0]

    def norm_sub(x_rows_ap, rows, xT_dst, r_xT, hT_dst, r_hT, g, sh):
        for grp in range(8):
            b = grp % 2
            if grp % 4 == 0:
                i = norm_ctr[0] % 2
                norm_ctr[0] += 1
                xs, xr = xs_buf[i], xs_res[i]
                hf = grp // 4
                Sx.dma("sp", lambda q, xs=xs, hf=hf: q.dma_start(
                    out=xs[0:rows, :], in_=x_rows_ap[:, hf * 2048:(hf + 1) * 2048]), "ld_x%d" % i, writes=(xr,))
            for k4 in range(4):
                n = grp * 4 + k4
                nl = n % 16
                Sx.op("pe", lambda e, nl=nl, k4=k4, b=b, xs=xs: e.transpose(
                    out=PB[b][:, k4 * 128:k4 * 128 + rows], in_=xs[0:rows, nl * 128:(nl + 1) * 128],
                    identity=identF[0:rows, 0:rows]), reads=(xr, r_cst), writes=(PR[b],))
            if STOP == "n1":
                continue
            src = PB[b][:, :].rearrange("p (a t) -> p a t", a=4)[:, :, 0:rows]
            Sx.op("dve", lambda e, grp=grp, src=src: e.tensor_copy(
                out=xT_dst[:, grp * 4:grp * 4 + 4, 0:rows], in_=src), reads=(PR[b],), writes=(r_xT,))
            if STOP == "n2":
                continue
            Sx.op("act", lambda e, grp=grp: e.activation(
                out=sq[:, grp * 4:grp * 4 + 4, 0:rows], in_=xT_dst[:, grp * 4:grp * 4 + 4, 0:rows], func=AF.Square),
                reads=(r_xT,), writes=(r_sq,))
        if STOP in ("n1", "n2", "n3"):
            return
        for n in range(KC):
            Sx.op("pe", lambda e, n=n: e.matmul(PB[2][:, 0:rows], lhsT=onesB, rhs=sq[:, n, 0:rows],
                                                start=(n == 0), stop=(n == KC - 1)),
                  reads=(r_sq, r_cb), writes=(PR[2],))
        if STOP == "n4":
            return
        Sx.op("act", lambda e: e.activation(out=rstd[:, 0:rows], in_=PB[2][:, 0:rows], func=AF.Sqrt,
                                            bias=epsn[:, 0:1], scale=1.0 / D),
              reads=(PR[2], r_cb), writes=(r_rstd,))
        Sx.op("dve", lambda e: e.reciprocal(out=rstd[:, 0:rows], in_=rstd[:, 0:rows]),
              reads=(r_rstd,), writes=(r_rstd,))
        if STOP == "n5":
            return
        for n in range(KC):
            t = ntmp[n % 2]
            tr = r_ntmp[n % 2]
            Sx.op("dve", lambda e, n=n, t=t: e.scalar_tensor_tensor(
                out=t[:, 0:rows], in0=xT_dst[:, n, 0:rows], scalar=g[:, n:n + 1], in1=rstd[:, 0:rows],
                op0=OP.mult, op1=OP.mult), reads=(r_xT, r_rstd, r_mod), writes=(tr,))
            Sx.op("act", lambda e, n=n, t=t: e.activation(
                out=hT_dst[:, n, 0:rows], in_=t[:, 0:rows], func=AF.Identity, bias=sh[:, n:n + 1],
                scale=1.0), reads=(tr, r_mod), writes=(r_hT,))

    st1 = contextlib.ExitStack()
    cur[0] = st1
    hT = sb("hT", [128, KC, TT], BF)
    r_hT = Res("hT")
    r_proj = Res("proj")
    xTs = big1[:, :].rearrange("p (k t) -> p k t", t=128)
    r_xTs = r_proj
    carry = sb("carry", [128, 16])
    r_carry = Res("carry")
    Sx.op("dve", lambda e: e.memset(carry[:, :], 0.0), writes=(r_carry,))
    Pb = [sb("Pb%d" % i, [128, TT + 1]) for i in range(2)]
    r_Pb = [Res("Pb%d" % i) for i in range(2)]
    dtmp = sb("dtmp", [128, TT])
    r_dtmp = Res("dtmp")
    rT = big1[:, 0:4 * TT].rearrange("p (c t) -> p c t", c=4)
    kT = big1[:, 2048:2048 + 4 * TT].rearrange("p (c t) -> p c t", c=4)
    vpad = sb("vpad", [128, 4, 64 + TT], BF)
    twd = sb("twd", [96, TT], BF)
    adT = sb("adT", [96, TT], BF)
    sgd = sb("sgd", [128, 2, 64 + TT], BF)
    rkT = sb("rkT", [128, 4, 64 + TT], BF)
    Sx.op("dve", lambda e: e.memset(vpad[:, :, 0:64], 0.0), writes=(r_proj,))
    Sx.op("dve", lambda e: e.memset(sgd[:, :, 0:64], 0.0), writes=(r_proj,))
    AR = sb("AR", [128, 4, NCH, 128], BF)
    BK = sb("BK", [128, 4, NCH, 128], BF)
    BKh = sb("BKh", [128, 4, NCH, 128], BF)
    Wc = sb("Wc", [128, 4, NCH])
    ARo = sb("ARo", [64, 4, NCH, 128], BF)
    BKo = sb("BKo", [64, 4, NCH, 128], BF)
    BKho = sb("BKho", [64, 4, NCH, 128], BF)
    vo = sb("vo", [64, 4, TT], BF)
    rko = sb("rko", [64, 4, TT], BF)
    Wco = sb("Wco", [64, 4, NCH])
    r_sh = Res("shift")
    Sx.newsem("sh", dma=True)
    r_prep = Res("prep")
    Sx.op("dve", lambda e: e.memset(rkT[:, :, 0:64], 0.0), writes=(r_prep,))
    tnames = ["sg", "Ls", "Er", "Ek", "Ea", "ic", "kq", "kkv", "kp", "t1"]
    tp = {n_: tpbuf[:, i_ * 512:i_ * 512 + TT] for i_, n_ in enumerate(tnames)}
    r_tp = {n_: (r_sq if i_ < 4 else Res("tp_" + n_)) for i_, n_ in enumerate(tnames)}
    sqk = sb("sqk", [128, TT], BF)
    r_sqk = Res("sqk")
    M32 = sb("M32", [64, 8, 64])
    Mb = sb("Mb", [64, 8, 64], BF)
    r_M = Res("M")
    Sx.op("dve", lambda e: e.memset(M32[:, :, :], 0.0), writes=(r_M,))
    Sx.op("dve", lambda e: e.memset(Mb[:, :, :], 0.0), writes=(r_M,))
    GTb = sb("GTb", [64, 8, 128], BF)
    GTk = sb("GTk", [64, 8, 128], BF)
    r_GTm = Res("GTm")
    Xb = [sb("Xb%d" % i, [64, 8, 64]) for i in range(2)]
    Xtb = [sb("Xtb%d" % i, [64, 8, 64]) for i in range(2)]
    r_Xb = [Res("Xb%d" % i) for i in range(2)]
    r_Xtb = [Res("Xtb%d" % i) for i in range(2)]
    Tt = sb("Tt", [64, 8, 64])
    r_Tt = Res("Tt")
    Psb = sb("Psb", [64, 8, 64])
    r_Psb = Res("Psb")
    Ut = sb("Ut", [64, 8, 64], BF)
    Vt = sb("Vt", [64, 8, 64], BF)
    BhT = sb("BhT", [64, 8, 64], BF)
    KhT = sb("KhT", [64, 8, 64], BF)
    r_U = Res("U")
    r_V = Res("V")
    r_BKhT = Res("BKhT")
    ysb = sb("ysb", [128, 8, 64])
    ysq = sb("ysq", [128, 8, 64])
    r_y = Res("y")
    st = sb("st", [128, 64])
    r_st = Res("st")
    ob = sb("ob", [128, 512], BF)
    r_ob = Res("ob")
    oTt = [sb("oTt%d" % i, [128, 4, TT], BF) for i in range(1)]
    r_oTt = [Res("oTt%d" % i) for i in range(1)]

    mu = vec1[:, 0:16]
    w0v, a0v, kkv_, kav, rkv = (vec1[:, 16 + 4 * i:20 + 4 * i] for i in range(5))

    def v3(ap, a):
        return ap.rearrange("p (a b) -> p a b", a=a)

    for t in range(NT1):
        issue_precast((len(precast) + NT1 - 1) // NT1)
        for s in range(NSUB):
            r0 = t * TT + s * 128
            norm_sub(x_d[r0:r0 + 128, :], 128, xTs, r_xTs, hT[:, :, s * 128:(s + 1) * 128], r_hT, g1, sh1)
        if STOP in ("norm", "n1", "n2", "n3", "n4", "n5"):
            return finish()
        for c in range(16):
            wt, wr = WS.next()
            wt3 = wt.rearrange("p (k c) -> p k c", c=128)
            Mw = 96 if c in (12, 13) else 128
            b = 3 + (c % 2)
            for kc in range(KC):
                Sx.op("pe", lambda e, kc=kc, wt3=wt3, Mw=Mw, b=b: e.matmul(
                    PB[b][0:Mw, 0:TT], lhsT=wt3[:, kc, 0:Mw], rhs=hT[:, kc, :],
                    start=(kc == 0), stop=(kc == KC - 1)), reads=(wr, r_hT), writes=(PR[b],))
            P_, rP = Pb[c % 2], r_Pb[c % 2]
            Sx.op("act", lambda e, P_=P_, Mw=Mw, b=b: e.activation(
                out=P_[0:Mw, 1:TT + 1], in_=PB[b][0:Mw, 0:TT], func=AF.Copy),
                reads=(PR[b],), writes=(rP,))
            Sx.op("dve", lambda e, P_=P_, Mw=Mw, c=c: e.tensor_copy(
                out=P_[0:Mw, 0:1], in_=carry[0:Mw, c:c + 1]), reads=(r_carry,), writes=(rP,))
            Sx.op("dve", lambda e, P_=P_, Mw=Mw: e.tensor_tensor(
                out=dtmp[0:Mw, :], in0=P_[0:Mw, 0:TT], in1=P_[0:Mw, 1:TT + 1], op=OP.subtract),
                reads=(rP,), writes=(r_dtmp,))
            Sx.op("dve", lambda e, P_=P_, Mw=Mw, c=c: e.tensor_copy(
                out=carry[0:Mw, c:c + 1], in_=P_[0:Mw, TT:TT + 1]), reads=(rP,), writes=(r_carry,))
            if c < 4:
                dst = rT[:, c, :]
            elif c < 8:
                dst = kT[:, c - 4, :]
            elif c < 12:
                dst = vpad[:, c - 8, 64:64 + TT]
            elif c == 12:
                dst = tp["t1"][0:96, :]
            elif c == 13:
                dst = adT[0:96, :]
            else:
                dst = tp["t1"]
            wres = (r_proj,) if c not in (12, 14, 15) else (r_tp["t1"],)
            Sx.op("dve", lambda e, P_=P_, Mw=Mw, c=c, dst=dst: e.scalar_tensor_tensor(
                out=dst, in0=dtmp[0:Mw, :], scalar=mu[0:Mw, c:c + 1], in1=P_[0:Mw, 1:TT + 1],
                op0=OP.mult, op1=OP.add), reads=(r_dtmp, rP, r_cst), writes=wres)
            if c == 12:
                Sx.op("act", lambda e: e.activation(out=twd[0:96, :], in_=tp["t1"][0:96, :], func=AF.Tanh),
                      reads=(r_tp["t1"],), writes=(r_proj,))
            if c in (14, 15):
                Sx.op("act", lambda e, c=c: e.activation(out=sgd[:, c - 14, 64:64 + TT], in_=tp["t1"],
                                                         func=AF.Sigmoid),
                      reads=(r_tp["t1"],), writes=(r_proj,))
        if STOP == "proj":
            return finish()
        for cc in range(4):
            cs = slice(cc * 128, (cc + 1) * 128)
            T_ = tp
            Sx.op("pe", lambda e, cs=cs: e.matmul(PB[5][:, 0:TT], lhsT=wld[0:96, cs], rhs=twd[0:96, :],
                                                  start=True, stop=True),
                  reads=(r_cb, r_proj), writes=(PR[5],))
            Sx.op("act", lambda e, cc=cc: e.activation(out=T_["sg"], in_=PB[5][:, 0:TT], func=AF.Sigmoid,
                                                       bias=w0v[:, cc:cc + 1], scale=1.0),
                  reads=(PR[5], r_cst), writes=(r_tp["sg"],))
            Sx.op("dve", lambda e: e.tensor_tensor_scan(out=T_["Ls"], data0=cmask, data1=T_["sg"],
                                                        initial=0.0, op0=OP.mult, op1=OP.add),
                  reads=(r_tp["sg"], r_cst), writes=(r_tp["Ls"],))
            Sx.op("act", lambda e: e.activation(out=T_["Er"], in_=T_["Ls"], func=AF.Exp, scale=-C0),
                  reads=(r_tp["Ls"],), writes=(r_tp["Er"],))
            Sx.op("act", lambda e: e.activation(out=T_["Ek"], in_=T_["Ls"], func=AF.Exp, scale=C0),
                  reads=(r_tp["Ls"],), writes=(r_tp["Ek"],))
            Sx.op("dve", lambda e: e.tensor_tensor(out=T_["t1"], in0=T_["Ls"], in1=T_["sg"],
                                                   op=OP.subtract),
                  reads=(r_tp["Ls"], r_tp["sg"]), writes=(r_tp["t1"],))
            Sx.op("act", lambda e: e.activation(out=T_["Ea"], in_=T_["t1"], func=AF.Exp, scale=-C0),
                  reads=(r_tp["t1"],), writes=(r_tp["Ea"],))
            Sx.op("dve", lambda e, cc=cc: e.tensor_copy(
                out=Wc[:, cc, :], in_=v3(T_["Er"], NCH)[:, :, 63]), reads=(r_tp["Er"],), writes=(r_prep,))
            Sx.op("pe", lambda e, cs=cs: e.matmul(PB[6][:, 0:TT], lhsT=wli[0:96, cs], rhs=adT[0:96, :],
                                                  start=True, stop=True),
                  reads=(r_cb, r_proj), writes=(PR[6],))
            Sx.op("act", lambda e, cc=cc: e.activation(out=T_["ic"], in_=PB[6][:, 0:TT], func=AF.Sigmoid,
                                                       bias=a0v[:, cc:cc + 1], scale=1.0),
                  reads=(PR[6], r_cst), writes=(r_tp["ic"],))
            Sx.op("dve", lambda e, cc=cc: e.tensor_scalar(out=T_["kq"], in0=kT[:, cc, :],
                                                          scalar1=kkv_[:, cc:cc + 1], scalar2=None, op0=OP.mult),
                  reads=(r_proj, r_cst), writes=(r_tp["kq"],))
            Sx.op("act", lambda e: e.activation(out=sqk[:, :], in_=T_["kq"], func=AF.Square),
                  reads=(r_tp["kq"],), writes=(r_sqk,))
            Sx.op("pe", lambda e: e.matmul(PB[7][:, 0:TT], lhsT=blkones, rhs=sqk[:, :], start=True, stop=True),
                  reads=(r_cb, r_sqk), writes=(PR[7],))
            Sx.op("act", lambda e: e.activation(out=T_["kkv"], in_=PB[7][:, 0:TT], func=AF.Sqrt),
                  reads=(PR[7],), writes=(r_tp["kkv"],))
            Sx.op("dve", lambda e: e.tensor_scalar(out=T_["kkv"], in0=T_["kkv"], scalar1=1e-12,
                                                   scalar2=None, op0=OP.max),
                  reads=(r_tp["kkv"],), writes=(r_tp["kkv"],))
            Sx.op("dve", lambda e: e.reciprocal(out=T_["kkv"], in_=T_["kkv"]),
                  reads=(r_tp["kkv"],), writes=(r_tp["kkv"],))
            Sx.op("dve", lambda e: e.tensor_tensor(out=T_["kkv"], in0=T_["kkv"], in1=T_["kq"],
                                                   op=OP.mult),
                  reads=(r_tp["kkv"], r_tp["kq"]), writes=(r_tp["kkv"],))
            Sx.op("dve", lambda e, cc=cc: e.tensor_scalar(out=T_["t1"], in0=T_["ic"], scalar1=-1.0,
                                                          scalar2=kav[:, cc:cc + 1], op0=OP.add, op1=OP.mult),
                  reads=(r_tp["ic"], r_cst), writes=(r_tp["t1"],))
            Sx.op("dve", lambda e, cc=cc: e.scalar_tensor_tensor(out=T_["kp"], in0=T_["t1"], scalar=1.0,
                                                                 in1=kT[:, cc, :], op0=OP.add, op1=OP.mult),
                  reads=(r_tp["t1"], r_proj), writes=(r_tp["kp"],))
            Sx.op("dve", lambda e, cc=cc: e.scalar_tensor_tensor(
                out=rkT[:, cc, 64:64 + TT], in0=rT[:, cc, :], scalar=rkv[:, cc:cc + 1], in1=T_["kp"],
                op0=OP.mult, op1=OP.mult), reads=(r_proj, r_tp["kp"], r_cst), writes=(r_prep,))
            Sx.op("dve", lambda e, cc=cc: e.scalar_tensor_tensor(
                out=AR[:, cc, :, 0:64], in0=v3(T_["kkv"], NCH), scalar=-1.0, in1=v3(T_["Ea"], NCH),
                op0=OP.mult, op1=OP.mult), reads=(r_tp["kkv"], r_tp["Ea"]), writes=(r_prep,))
            Sx.op("dve", lambda e, cc=cc: e.tensor_tensor(
                out=AR[:, cc, :, 64:128], in0=v3(rT[:, cc, :], NCH), in1=v3(T_["Er"], NCH), op=OP.mult),
                reads=(r_proj, r_tp["Er"]), writes=(r_prep,))
            Sx.op("dve", lambda e: e.tensor_tensor(out=T_["t1"], in0=T_["kkv"], in1=T_["ic"],
                                                   op=OP.mult),
                  reads=(r_tp["kkv"], r_tp["ic"]), writes=(r_tp["t1"],))
            Sx.op("dve", lambda e, cc=cc: e.tensor_tensor(
                out=BK[:, cc, :, 0:64], in0=v3(T_["t1"], NCH), in1=v3(T_["Ek"], NCH), op=OP.mult),
                reads=(r_tp["t1"], r_tp["Ek"]), writes=(r_prep,))
            Sx.op("dve", lambda e, cc=cc: e.tensor_tensor(
                out=BK[:, cc, :, 64:128], in0=v3(T_["kp"], NCH), in1=v3(T_["Ek"], NCH), op=OP.mult),
                reads=(r_tp["kp"], r_tp["Ek"]), writes=(r_prep,))
            for c in range(NCH):
                Sx.op("dve", lambda e, cc=cc, c=c: e.tensor_scalar(
                    out=BKh[:, cc, c, :], in0=BK[:, cc, c, :], scalar1=Wc[:, cc, c:c + 1], scalar2=None,
                    op0=OP.mult), reads=(r_prep,), writes=(r_prep,))
        if STOP == "prep":
            return finish()
        for dst_, src_ in ((ARo, AR), (BKo, BK), (BKho, BKh)):
            Sx.dma("sp", lambda g, dst_=dst_, src_=src_: g.dma_start(out=dst_[:, :, :, :], in_=src_[64:128, :, :, :]),
                   "sh", reads=(r_prep,), writes=(r_sh,))
        Sx.dma("sp", lambda g: g.dma_start(out=vo[:, :, :], in_=vpad[64:128, :, 64:64 + TT]), "sh",
               reads=(r_proj,), writes=(r_sh,))
        Sx.dma("sp", lambda g: g.dma_start(out=rko[:, :, :], in_=rkT[64:128, :, 64:64 + TT]), "sh",
               reads=(r_prep,), writes=(r_sh,))
        Sx.dma("sp", lambda g: g.dma_start(out=Wco[:, :, :], in_=Wc[64:128, :, :]), "sh",
               reads=(r_prep,), writes=(r_sh,))
        lo = slice(0, 64)

        def fAR(h, c, a, b_):
            return (AR if h % 2 == 0 else ARo)[lo, h // 2, c, a:b_]

        def fBK(h, c, a, b_):
            return (BK if h % 2 == 0 else BKo)[lo, h // 2, c, a:b_]

        def fBKh(h, c, a, b_):
            return (BKh if h % 2 == 0 else BKho)[lo, h // 2, c, a:b_]

        def fV(h, c):
            return vpad[lo, h // 2, 64 + c * 64:128 + c * 64] if h % 2 == 0 else vo[lo, h // 2, c * 64:(c + 1) * 64]

        def fRK(h, c):
            return rkT[lo, h // 2, 64 + c * 64:128 + c * 64] if h % 2 == 0 else rko[lo, h // 2, c * 64:(c + 1) * 64]

        def fWc(h, c):
            return (Wc if h % 2 == 0 else Wco)[lo, h // 2, c:c + 1]
        RD = (r_prep, r_proj, r_sh)
        idB = identB[lo, lo]
        ot, r_ot = oTt[0], r_oTt[0]
        for c in range(NCH):
            for h in range(8):
                Sx.op("pe", lambda e, h=h: e.matmul(
                    PB[h // 4][lo, (h % 4) * 128:(h % 4 + 1) * 128], lhsT=fBK(h, c, 0, 64), rhs=fAR(h, c, 0, 128),
                    start=True, stop=True), reads=RD, writes=(PR[h // 4],))
            for h in range(8):
                Sx.op("pe", lambda e, h=h: e.matmul(
                    PB[2 + h // 4][lo, (h % 4) * 128:(h % 4 + 1) * 128], lhsT=fBK(h, c, 64, 128), rhs=fAR(h, c, 0, 128),
                    start=True, stop=True), reads=RD, writes=(PR[2 + h // 4],))
            for h in range(8):
                Sx.op("pe", lambda e, h=h: e.matmul(
                    PB[4][lo, h * 64:(h + 1) * 64], lhsT=fAR(h, c, 0, 64), rhs=fBK(h, c, 0, 64),
                    start=True, stop=True), reads=RD, writes=(PR[4],))
            for h in range(8):
                Sx.op("pe", lambda e, h=h: e.transpose(out=pbf(5)[lo, h * 64:(h + 1) * 64], in_=fBKh(h, c, 0, 64),
                                                       identity=idB), reads=RD + (r_cb,), writes=(PR[5],))
            for h in range(8):
                Sx.op("pe", lambda e, h=h: e.transpose(out=pbf(5)[lo, 512 + h * 64:512 + (h + 1) * 64],
                                                       in_=fBKh(h, c, 64, 128), identity=idB),
                      reads=RD + (r_cb,), writes=(PR[5],))
            for h in range(8):
                Sx.op("pe", lambda e, h=h: e.transpose(out=pbf(6)[lo, h * 64:(h + 1) * 64], in_=fV(h, c),
                                                       identity=idB), reads=RD + (r_cb,), writes=(PR[6],))
            Sx.op("act", lambda e: e.activation(out=BhT[:, :, :], in_=v3(pbf(5)[lo, 0:512], 8), func=AF.Copy),
                  reads=(PR[5],), writes=(r_BKhT,))
            Sx.op("act", lambda e: e.activation(out=KhT[:, :, :], in_=v3(pbf(5)[lo, 512:1024], 8), func=AF.Copy),
                  reads=(PR[5],), writes=(r_BKhT,))
            Sx.op("act", lambda e: e.activation(out=Vt[:, :, :], in_=v3(pbf(6)[lo, 0:512], 8), func=AF.Copy),
                  reads=(PR[6],), writes=(r_V,))
            if STOP == "s0":
                return finish()
            for b in range(2):
                Sx.op("dve", lambda e, b=b: e.tensor_tensor(
                    out=GTb[:, 4 * b:4 * b + 4, :], in0=v3(PB[b][lo, :], 4), in1=v3(maskGT[lo, :], 4), op=OP.mult),
                    reads=(PR[b], r_cst), writes=(r_GTm,))
                Sx.op("dve", lambda e, b=b: e.tensor_tensor(
                    out=GTk[:, 4 * b:4 * b + 4, :], in0=v3(PB[2 + b][lo, :], 4), in1=v3(maskGT[lo, :], 4), op=OP.mult),
                    reads=(PR[2 + b], r_cst), writes=(r_GTm,))
                Sx.op("dve", lambda e, b=b: e.tensor_tensor(
                    out=Xtb[0][:, 4 * b:4 * b + 4, :], in0=v3(PB[b][lo, :], 4)[:, :, 0:64],
                    in1=v3(maskGT[lo, :], 4)[:, :, 0:64], op=OP.mult),
                    reads=(PR[b], r_cst), writes=(r_Xtb[0],))
            Sx.op("dve", lambda e: e.tensor_tensor(out=Xb[0][:, :, :], in0=v3(PB[4][lo, :], 8),
                                                   in1=v3(maskL8, 8), op=OP.mult),
                  reads=(PR[4], r_cst), writes=(r_Xb[0],))
            Sx.op("dve", lambda e: e.tensor_tensor(out=Tt[:, :, :], in0=Xtb[0][:, :, :], in1=v3(ident8, 8),
                                                   op=OP.add), reads=(r_Xtb[0], r_cst), writes=(r_Tt,))
            if STOP == "s1":
                return finish()
            for k in range(5):
                a, bnx = k % 2, (k + 1) % 2
                for h in range(8):
                    Sx.op("pe", lambda e, h=h, a=a: e.matmul(
                        PB[0][lo, h * 64:(h + 1) * 64], lhsT=Xtb[a][:, h, :], rhs=Xb[a][:, h, :],
                        start=True, stop=True), reads=(r_Xtb[a], r_Xb[a]), writes=(PR[0],))
                if k < 4:
                    for h in range(8):
                        Sx.op("pe", lambda e, h=h, a=a: e.matmul(
                            PB[1][lo, h * 64:(h + 1) * 64], lhsT=Xb[a][:, h, :], rhs=Xtb[a][:, h, :],
                            start=True, stop=True), reads=(r_Xtb[a], r_Xb[a]), writes=(PR[1],))
                Sx.op("act", lambda e, bnx=bnx: e.activation(out=Xb[bnx][:, :, :], in_=v3(PB[0][lo, :], 8),
                                                             func=AF.Copy), reads=(PR[0],), writes=(r_Xb[bnx],))
                if k < 4:
                    Sx.op("dve", lambda e, bnx=bnx: e.tensor_copy(out=Xtb[bnx][:, :, :], in_=v3(PB[1][lo, :], 8)),
                          reads=(PR[1],), writes=(r_Xtb[bnx],))
                for h in range(8):
                    Sx.op("pe", lambda e, h=h, bnx=bnx: e.matmul(
                        PB[7][lo, h * 64:(h + 1) * 64], lhsT=Xb[bnx][:, h, :], rhs=Tt[:, h, :],
                        start=True, stop=True), reads=(r_Xb[bnx], r_Tt), writes=(PR[7],))
                Sx.op("dve", lambda e: e.tensor_tensor(out=Tt[:, :, :], in0=Tt[:, :, :], in1=v3(PB[7][lo, :], 8),
                                                       op=OP.add), reads=(PR[7], r_Tt), writes=(r_Tt,))
            if STOP == "s2":
                return finish()
            for h in range(8):
                Sx.op("pe", lambda e, h=h: e.matmul(
                    PB[2][lo, h * 64:(h + 1) * 64], lhsT=fAR(h, c, 0, 64), rhs=Mb[:, h, :],
                    start=True, stop=False), reads=RD + (r_M,), writes=(PR[2],))
                Sx.op("pe", lambda e, h=h: e.matmul(
                    PB[2][lo, h * 64:(h + 1) * 64], lhsT=GTk[:, h, 0:64], rhs=Vt[:, h, :],
                    start=False, stop=True), reads=(r_GTm, r_V), writes=(PR[2],))
            Sx.op("act", lambda e: e.activation(out=Psb[:, :, :], in_=v3(PB[2][lo, :], 8), func=AF.Copy),
                  reads=(PR[2],), writes=(r_Psb,))
            for h in range(8):
                Sx.op("pe", lambda e, h=h: e.matmul(
                    PB[6][lo, h * 64:(h + 1) * 64], lhsT=Tt[:, h, :], rhs=Psb[:, h, :], start=True, stop=True),
                    reads=(r_Tt, r_Psb), writes=(PR[6],))
            Sx.op("dve", lambda e: e.tensor_copy(out=Ut[:, :, :], in_=v3(PB[6][lo, :], 8)),
                  reads=(PR[6],), writes=(r_U,))
            for h in range(8):
                Sx.op("pe", lambda e, h=h: e.matmul(
                    PB[7][lo, h * 64:(h + 1) * 64], lhsT=fAR(h, c, 64, 128), rhs=Mb[:, h, :],
                    start=True, stop=False), reads=RD + (r_M,), writes=(PR[7],))
                Sx.op("pe", lambda e, h=h: e.matmul(
                    PB[7][lo, h * 64:(h + 1) * 64], lhsT=GTb[:, h, 64:128], rhs=Ut[:, h, :],
                    start=False, stop=False), reads=(r_GTm, r_U), writes=(PR[7],))
                Sx.op("pe", lambda e, h=h: e.matmul(
                    PB[7][lo, h * 64:(h + 1) * 64], lhsT=GTk[:, h, 64:128], rhs=Vt[:, h, :],
                    start=False, stop=True), reads=(r_GTm, r_V), writes=(PR[7],))
            for h in range(8):
                Sx.op("pe", lambda e, h=h: e.matmul(
                    PB[3][lo, h * 64:(h + 1) * 64], lhsT=BhT[:, h, :], rhs=Ut[:, h, :],
                    start=True, stop=False), reads=(r_BKhT, r_U), writes=(PR[3],))
                Sx.op("pe", lambda e, h=h: e.matmul(
                    PB[3][lo, h * 64:(h + 1) * 64], lhsT=KhT[:, h, :], rhs=Vt[:, h, :],
                    start=False, stop=True), reads=(r_BKhT, r_V), writes=(PR[3],))
            for h in range(8):
                Sx.op("dve", lambda e, h=h: e.scalar_tensor_tensor(
                    out=M32[:, h, :], in0=M32[:, h, :], scalar=fWc(h, c),
                    in1=PB[3][lo, h * 64:(h + 1) * 64], op0=OP.mult, op1=OP.add),
                    reads=(r_M, r_prep, r_sh, PR[3]), writes=(r_M,))
            Sx.op("act", lambda e: e.activation(out=Mb[:, :, :], in_=M32[:, :, :], func=AF.Copy),
                  reads=(r_M,), writes=(r_M,))
            if STOP == "s3":
                return finish()
            hi = lo
            Sx.op("act", lambda e: e.activation(out=ysb[hi, :, :], in_=v3(PB[7][hi, :], 8), func=AF.Copy),
                  reads=(PR[7],), writes=(r_y,))
            Sx.op("act", lambda e: e.activation(out=ysq[hi, :, :], in_=v3(PB[7][hi, :], 8), func=AF.Square),
                  reads=(PR[7],), writes=(r_y,))
            Sx.op("dve", lambda e: e.tensor_reduce(out=st[hi, 0:8], in_=ysb[hi, :, :], axis=AX.X, op=OP.add),
                  reads=(r_y,), writes=(r_st,))
            Sx.op("dve", lambda e: e.tensor_reduce(out=st[hi, 8:16], in_=ysq[hi, :, :], axis=AX.X, op=OP.add),
                  reads=(r_y,), writes=(r_st,))
            Sx.op("dve", lambda e: e.tensor_scalar(out=st[hi, 16:24], in0=st[hi, 0:8], scalar1=1.0 / HD,
                                                   scalar2=None, op0=OP.mult), reads=(r_st,), writes=(r_st,))
            Sx.op("dve", lambda e: e.tensor_tensor(out=st[hi, 24:32], in0=st[hi, 16:24], in1=st[hi, 16:24],
                                                   op=OP.mult), reads=(r_st,), writes=(r_st,))
            Sx.op("dve", lambda e: e.scalar_tensor_tensor(out=st[hi, 32:40], in0=st[hi, 8:16], scalar=1.0 / HD,
                                                          in1=st[hi, 24:32], op0=OP.mult, op1=OP.subtract),
                  reads=(r_st,), writes=(r_st,))
            Sx.op("act", lambda e: e.activation(out=st[hi, 40:48], in_=st[hi, 32:40], func=AF.Sqrt,
                                                bias=epsn[hi, 1:2], scale=1.0), reads=(r_st, r_cb), writes=(r_st,))
            Sx.op("dve", lambda e: e.reciprocal(out=st[hi, 40:48], in_=st[hi, 40:48]), reads=(r_st,), writes=(r_st,))
            for h in range(8):
                Sx.op("dve", lambda e, h=h: e.tensor_scalar(
                    out=ysb[hi, h, :], in0=ysb[hi, h, :], scalar1=st[hi, 16 + h:17 + h], scalar2=st[hi, 40 + h:41 + h],
                    op0=OP.subtract, op1=OP.mult), reads=(r_y, r_st), writes=(r_y,))
            Sx.op("dve", lambda e: e.tensor_tensor(out=ysb[hi, :, :], in0=ysb[hi, :, :], in1=v3(lnx[hi, 0:512], 8),
                                                   op=OP.mult), reads=(r_y, r_cst), writes=(r_y,))
            Sx.op("dve", lambda e: e.tensor_tensor(out=ysb[hi, :, :], in0=ysb[hi, :, :], in1=v3(lnx[hi, 512:1024], 8),
                                                   op=OP.add), reads=(r_y, r_cst), writes=(r_y,))
            for h in range(8):
                Sx.op("pe", lambda e, h=h: e.matmul(
                    PB[1][lo, 2 * h:2 * h + 2], lhsT=fRK(h, c), rhs=onesB[lo, 0:2],
                    start=True, stop=True), reads=RD + (r_cb,), writes=(PR[1],))
            Sx.op("act", lambda e: e.activation(out=st[hi, 48:56], in_=v3(PB[1][hi, 0:16], 8)[:, :, 0], func=AF.Copy),
                  reads=(PR[1],), writes=(r_st,))
            for h in range(8):
                Sx.op("dve", lambda e, h=h: e.scalar_tensor_tensor(
                    out=ysb[hi, h, :], in0=Vt[:, h, :], scalar=st[hi, 48 + h:49 + h], in1=ysb[hi, h, :],
                    op0=OP.mult, op1=OP.add), reads=(r_V, r_st, r_y), writes=(r_y,))
            for k2 in range(2):
                Sx.op("pe", lambda e, k2=k2: e.matmul(
                    PB[0][lo, 0:512], lhsT=sgd[:, k2, 64 + c * 64:128 + c * 64], rhs=wlg[:, k2 * 512:(k2 + 1) * 512],
                    start=(k2 == 0), stop=(k2 == 1)), reads=(r_proj, r_cb), writes=(PR[0],))
            Sx.op("dve", lambda e: e.tensor_tensor(out=ob[hi, :], in0=ysb[hi, :, :].rearrange("p a b -> p (a b)"),
                                                   in1=PB[0][hi, 0:512], op=OP.mult),
                  reads=(r_y, PR[0]), writes=(r_ob,))
            for cc in range(4):
                Sx.op("pe", lambda e, cc=cc: e.transpose(
                    out=pbf(4)[:, cc * 64:(cc + 1) * 64], in_=ob[hi, cc * 128:(cc + 1) * 128],
                    identity=idB), reads=(r_ob, r_cb), writes=(PR[4],))
            Sx.op("act", lambda e: e.activation(out=ot[:, :, c * 64:(c + 1) * 64], in_=v3(pbf(4)[:, 0:256], 4),
                                                func=AF.Copy), reads=(PR[4],), writes=(r_ot,))
        q, tq = (t * TT) // TQ, (t * TT) % TQ
        dst = oT_loc[q].ap().bitcast(BF).rearrange("(c p) t -> p c t", p=128)[:, :, tq:tq + TT]
        Sx.dma("sp", lambda g, dst=dst, ot=ot: g.dma_start(out=dst, in_=ot[:, :, :]), "st_o", reads=(r_ot,))
        if dbg:
            dsto = dbg_o.rearrange("(c p) t -> p c t", p=128)[:, :, t * TT:(t + 1) * TT]
            Sx.dma("sp", lambda g, dsto=dsto, ot=ot: g.dma_start(out=dsto, in_=ot[:, :, :]), "st_o", reads=(r_ot,))

    if not phase2:
        return finish()
    Sx.wait_all("pool", ["st_o"])
    for q in range(4):
        ins = nc.gpsimd.collective_compute("AllGather", OP.bypass, replica_groups=[list(range(NCORE))],
                                           ins=[oT_loc[q].ap().opt()], outs=[og[q].ap().opt()])
        ins.then_inc(Sx.sem["cc"], 1)
        Sx.cnt["cc"] += 1
    r_og = Res("og")
    r_og.w = ("cc", Sx.cnt["cc"])
    barrier()
    st1.close()
    st2 = contextlib.ExitStack()
    cur[0] = st2
    N2 = TT2
    xT = sb("xT", [128, KC, N2])
    r_xT = Res("xT")
    hT2 = sb("hT2", [128, KC, N2], BF)
    r_hT2 = Res("hT2")
    oraw = sb("oraw", [128, 8, N2], BF)
    r_oraw = Res("oraw")
    Sx.newsem("ld_og", dma=True)
    oT2 = sb("oT2", [128, 16, N2], BF)
    r_oT2 = Res("oT2")
    ybin = sb("ybin", [128, 16, N2], BF)
    r_ybin = Res("ybin")
    orow = sb("orow", [128, D])
    r_merged = Res("merged")
    r_orow = r_merged
    merged = orow[:, :].bitcast(BF).rearrange("p (k t) -> p k t", t=N2)
    actb = sb("actb", [128, 11, N2], BF)
    r_actb = Res("actb")
    zc = tpbuf[:, 2000:2032].rearrange("p (m two) -> p m two", two=2)
    fc = tpbuf[:, 2100:2100 + 2 * NFF].rearrange("p (m two) -> p m two", two=2)
    r_zc = Res("zc")
    r_fc = Res("fc")
    Sx.op("dve", lambda e: e.memset(zc[:, :, :], 0.0), writes=(r_zc,))
    Sx.op("dve", lambda e: e.memset(fc[:, :, :], 0.0), writes=(r_fc,))
    NTMP = 6
    t2 = [tpbuf[:, i * 260:i * 260 + N2 + 2] for i in range(NTMP)]
    r_t2 = [Res("t2_%d" % i) for i in range(NTMP)]
    sq2 = [sb("sq2_%d" % i, [128, N2], BF) for i in range(2)]
    r_sq2 = [Res("sq2_%d" % i) for i in range(2)]
    rstd2 = tpbuf[:, 1600:1600 + N2]
    r_rstd2 = Res("rstd2")
    Sx.newsem("st_out", dma=True)
    cwm = vec2[:, 0:48]
    cwf = vec2[:, 48:48 + 3 * NFF]
    selv = vec2[:, 48 + 3 * NFF:48 + 3 * NFF + 8]
    selh = vec2[:, 48 + 3 * NFF + 8:48 + 3 * NFF + 16]
    hv = vec2[:, 48 + 3 * NFF + 16:48 + 3 * NFF + 17]

    def acc(bank, nk, rhs_fn, nt, rreads, Mw=128):
        wt, wr = WS.next()
        wt3 = wt.rearrange("p (k c) -> p k c", c=128)
        for kc in range(nk):
            Sx.op("pe", lambda e, kc=kc: e.matmul(PB[bank][0:Mw, 0:nt], lhsT=wt3[:, kc, 0:Mw], rhs=rhs_fn(kc),
                                                  start=(kc == 0), stop=(kc == nk - 1)),
                  reads=(wr,) + tuple(rreads), writes=(PR[bank],))

    def rms_bcast(nt, r_dst):
        for n in range(KC):
            q_, rq = sq2[n % 2], r_sq2[n % 2]
            Sx.op("act", lambda e, n=n, q_=q_: e.activation(out=q_[:, 0:nt], in_=xT[:, n, 0:nt], func=AF.Square),
                  reads=(r_xT,), writes=(rq,))
            Sx.op("pe", lambda e, n=n, q_=q_: e.matmul(PB[2][:, 0:nt], lhsT=onesB, rhs=q_[:, 0:nt],
                                                       start=(n == 0), stop=(n == KC - 1)),
                  reads=(rq, r_cb), writes=(PR[2],))
        Sx.op("act", lambda e: e.activation(out=rstd2[:, 0:nt], in_=PB[2][:, 0:nt], func=AF.Sqrt,
                                            bias=epsn[:, 0:1], scale=1.0 / D), reads=(PR[2], r_cb), writes=(r_dst,))
        Sx.op("dve", lambda e: e.reciprocal(out=rstd2[:, 0:nt], in_=rstd2[:, 0:nt]), reads=(r_dst,), writes=(r_dst,))

    def conv3(dst_fn, buf, nt, w_ap, widx, stride, rbuf, rd, wr_):
        ta, tb = t2[4], t2[5]
        Sx.op("dve", lambda e: e.tensor_scalar(out=ta[:, 0:nt], in0=buf[:, 0:nt], scalar1=w_ap[:, widx:widx + 1],
                                               scalar2=None, op0=OP.mult), reads=(rbuf, r_cst), writes=(r_t2[4],))
        Sx.op("dve", lambda e: e.scalar_tensor_tensor(out=tb[:, 0:nt], in0=buf[:, 1:nt + 1],
                                                      scalar=w_ap[:, widx + stride:widx + stride + 1], in1=ta[:, 0:nt],
                                                      op0=OP.mult, op1=OP.add),
              reads=(rbuf, r_t2[4], r_cst), writes=(r_t2[5],))
        Sx.op("dve", lambda e: e.scalar_tensor_tensor(out=dst_fn, in0=buf[:, 2:nt + 2],
                                                      scalar=w_ap[:, widx + 2 * stride:widx + 2 * stride + 1],
                                                      in1=tb[:, 0:nt], op0=OP.mult, op1=OP.add),
              reads=(rbuf, r_t2[5], r_cst) + tuple(rd), writes=tuple(wr_))

    def p2_tile(row0, nt, halo, col0, selvec):
        nsub = (nt + 127) // 128
        for s_ in range(nsub):
            rows = min(128, nt - s_ * 128)
            norm_sub(x2_d[row0 + s_ * 128:row0 + s_ * 128 + rows, :], rows, xT[:, :, s_ * 128:s_ * 128 + rows], r_xT,
                     hT2[:, :, s_ * 128:s_ * 128 + rows], r_hT2, g1, sh1)
        for k4 in range(16):
            for q8 in range(8):
                q, bb = q8 // 2, q8 % 2
                src = og[q].ap().bitcast(BF).rearrange("(k p) t -> p k t", p=128)[:, bb * 16 + k4, col0:col0 + nt]
                Sx.dma("sp", lambda g, q8=q8, src=src: g.dma_start(out=oraw[:, q8, 0:nt], in_=src), "ld_og",
                       reads=(r_og,), writes=(r_oraw,))
            dst = oT2[:, k4, 0:nt]
            Sx.op("dve", lambda e, dst=dst: e.tensor_scalar(out=dst, in0=oraw[:, 0, 0:nt], scalar1=selvec[:, 0:1],
                                                            scalar2=None, op0=OP.mult),
                  reads=(r_oraw, r_cst), writes=(r_oT2,))
            for q8 in range(1, 8):
                Sx.op("dve", lambda e, dst=dst, q8=q8: e.scalar_tensor_tensor(
                    out=dst, in0=oraw[:, q8, 0:nt], scalar=selvec[:, q8:q8 + 1], in1=dst, op0=OP.mult, op1=OP.add),
                    reads=(r_oraw, r_oT2, r_cst), writes=(r_oT2,))
        for m in range(16):
            for k in range(3):
                acc(3 + k, KC, lambda kc: hT2[:, kc, 0:nt], nt, (r_hT2,))
            zb, rzb = t2[0], r_t2[0]
            Sx.op("act", lambda e: e.activation(out=t2[1][:, 0:nt], in_=PB[4][:, 0:nt], func=AF.Copy),
                  reads=(PR[4],), writes=(r_t2[1],))
            Sx.op("dve", lambda e, m=m: e.tensor_copy(out=zb[:, 0:2], in_=zc[:, m, :]), reads=(r_zc,), writes=(rzb,))
            Sx.op("dve", lambda e: e.tensor_tensor(out=zb[:, 2:nt + 2], in0=t2[1][:, 0:nt], in1=PB[5][:, 0:nt],
                                                   op=OP.mult), reads=(r_t2[1], PR[5]), writes=(rzb,))
            if halo:
                Sx.op("dve", lambda e, m=m: e.tensor_scalar(out=zc[:, m, :], in0=zb[:, nt:nt + 2], scalar1=hv,
                                                            scalar2=None, op0=OP.mult),
                      reads=(rzb, r_cst), writes=(r_zc,))
            else:
                Sx.op("dve", lambda e, m=m: e.tensor_copy(out=zc[:, m, :], in_=zb[:, nt:nt + 2]),
                      reads=(rzb,), writes=(r_zc,))
            conv3(t2[2][:, 0:nt], zb, nt, cwm, m, 16, rzb, (), (r_t2[2],))
            Sx.op("dve", lambda e, m=m: e.tensor_tensor(out=ybin[:, m, 0:nt], in0=t2[2][:, 0:nt], in1=PB[3][:, 0:nt],
                                                        op=OP.mult), reads=(r_t2[2], PR[3]), writes=(r_ybin,))
        for n in range(KC):
            acc(3, KC, lambda kc: hT2[:, kc, 0:nt], nt, (r_hT2,))
            acc(4, KC, lambda kc: hT2[:, kc, 0:nt], nt, (r_hT2,))
            acc(5, 16, lambda kc: oT2[:, kc, 0:nt], nt, (r_oT2,))
            acc(6, 16, lambda kc: ybin[:, kc, 0:nt], nt, (r_ybin,))
            Sx.op("act", lambda e: e.activation(out=t2[0][:, 0:nt], in_=PB[3][:, 0:nt], func=AF.Sigmoid),
                  reads=(PR[3],), writes=(r_t2[0],))
            Sx.op("act", lambda e: e.activation(out=t2[1][:, 0:nt], in_=PB[4][:, 0:nt], func=AF.Sigmoid),
                  reads=(PR[4],), writes=(r_t2[1],))
            Sx.op("dve", lambda e: e.tensor_tensor(out=t2[0][:, 0:nt], in0=t2[0][:, 0:nt], in1=PB[5][:, 0:nt],
                                                   op=OP.mult), reads=(r_t2[0], PR[5]), writes=(r_t2[0],))
            Sx.op("dve", lambda e: e.tensor_tensor(out=t2[1][:, 0:nt], in0=t2[1][:, 0:nt], in1=PB[6][:, 0:nt],
                                                   op=OP.mult), reads=(r_t2[1], PR[6]), writes=(r_t2[1],))
            Sx.op("dve", lambda e, n=n: e.tensor_tensor(out=merged[:, n, 0:nt], in0=t2[0][:, 0:nt], in1=t2[1][:, 0:nt],
                                                        op=OP.add), reads=(r_t2[0], r_t2[1]), writes=(r_merged,))
        for n in range(KC):
            b = 3 + n % 2
            acc(b, KC, lambda kc: merged[:, kc, 0:nt], nt, (r_merged,))
            Sx.op("dve", lambda e, n=n, b=b: e.scalar_tensor_tensor(
                out=xT[:, n, 0:nt], in0=PB[b][:, 0:nt], scalar=gate1[:, n:n + 1], in1=xT[:, n, 0:nt],
                op0=OP.mult, op1=OP.add), reads=(PR[b], r_xT, r_mod), writes=(r_xT,))
        rms_bcast(nt, r_rstd2)
        for n in range(KC):
            ti, tr = t2[n % 2], r_t2[n % 2]
            Sx.op("dve", lambda e, n=n, ti=ti: e.scalar_tensor_tensor(
                out=ti[:, 0:nt], in0=xT[:, n, 0:nt], scalar=g2[:, n:n + 1], in1=rstd2[:, 0:nt],
                op0=OP.mult, op1=OP.mult), reads=(r_xT, r_rstd2, r_mod), writes=(tr,))
            Sx.op("act", lambda e, n=n, ti=ti: e.activation(out=hT2[:, n, 0:nt], in_=ti[:, 0:nt], func=AF.Identity,
                                                            bias=sh2[:, n:n + 1], scale=1.0),
                  reads=(tr, r_mod), writes=(r_hT2,))
        m0 = 0
        for gsz in FFG:
            for mi in range(gsz):
                m = m0 + mi
                acc(3, KC, lambda kc: hT2[:, kc, 0:nt], nt, (r_hT2,))
                if not halo:
                    acc(4, KC, lambda kc: hT2[:, kc, 0:nt], nt, (r_hT2,))
                ug, rug = t2[3], r_t2[3]
                Sx.op("act", lambda e: e.activation(out=ug[:, 2:nt + 2], in_=PB[3][:, 0:nt], func=AF.Copy),
                      reads=(PR[3],), writes=(rug,))
                Sx.op("dve", lambda e, m=m: e.tensor_copy(out=ug[:, 0:2], in_=fc[:, m, :]), reads=(r_fc,), writes=(rug,))
                if halo:
                    Sx.op("dve", lambda e, m=m: e.tensor_scalar(out=fc[:, m, :], in0=ug[:, nt:nt + 2], scalar1=hv,
                                                                scalar2=None, op0=OP.mult),
                          reads=(rug, r_cst), writes=(r_fc,))
                    continue
                Sx.op("dve", lambda e, m=m: e.tensor_copy(out=fc[:, m, :], in_=ug[:, nt:nt + 2]),
                      reads=(rug,), writes=(r_fc,))
                conv3(t2[2][:, 0:nt], ug, nt, cwf, m, NFF, rug, (), (r_t2[2],))
                Sx.op("act", lambda e: e.activation(out=t2[1][:, 0:nt], in_=t2[2][:, 0:nt], func=AF.Silu),
                      reads=(r_t2[2],), writes=(r_t2[1],))
                Sx.op("dve", lambda e, mi=mi: e.tensor_tensor(out=actb[:, mi, 0:nt], in0=t2[1][:, 0:nt],
                                                              in1=PB[4][:, 0:nt], op=OP.mult),
                      reads=(r_t2[1], PR[4]), writes=(r_actb,))
            if not halo:
                for nn in range(KC):
                    b = 5 + nn % 2
                    acc(b, gsz, lambda kc: actb[:, kc, 0:nt], nt, (r_actb,))
                    Sx.op("dve", lambda e, nn=nn, b=b: e.scalar_tensor_tensor(
                        out=xT[:, nn, 0:nt], in0=PB[b][:, 0:nt], scalar=gate2[:, nn:nn + 1], in1=xT[:, nn, 0:nt],
                        op0=OP.mult, op1=OP.add), reads=(PR[b], r_xT, r_mod), writes=(r_xT,))
            m0 += gsz
        if halo:
            return
        rms_bcast(nt, r_rstd2)
        for n in range(KC):
            Sx.op("dve", lambda e, n=n: e.scalar_tensor_tensor(
                out=xT[:, n, 0:nt], in0=xT[:, n, 0:nt], scalar=fgain[:, n:n + 1], in1=rstd2[:, 0:nt],
                op0=OP.mult, op1=OP.mult), reads=(r_xT, r_rstd2, r_cst), writes=(r_xT,))
        for s_ in range(nsub):
            for grp in range(8):
                b = grp % 2
                for k4 in range(4):
                    n = grp * 4 + k4
                    Sx.op("pe", lambda e, n=n, k4=k4, b=b: e.transpose(
                        out=PB[b][:, k4 * 128:(k4 + 1) * 128], in_=xT[:, n, s_ * 128:(s_ + 1) * 128],
                        identity=identF), reads=(r_xT, r_cst), writes=(PR[b],))
                eng = "act" if grp % 2 == 0 else "dve"
                if eng == "act":
                    Sx.op("act", lambda e, grp=grp, b=b: e.activation(out=orow[:, grp * 512:(grp + 1) * 512],
                                                                      in_=PB[b][:, :], func=AF.Copy),
                          reads=(PR[b],), writes=(r_orow,))
                else:
                    Sx.op("dve", lambda e, grp=grp, b=b: e.tensor_copy(out=orow[:, grp * 512:(grp + 1) * 512],
                                                                       in_=PB[b][:, :]),
                          reads=(PR[b],), writes=(r_orow,))
            r_out0 = row0 - 4 + s_ * 128
            Sx.dma("sp", lambda g, r_out0=r_out0: g.dma_start(out=out_d[r_out0:r_out0 + 128, :], in_=orow[:, :]),
                   "st_out", reads=(r_orow,))

    p2_tile(0, 4, True, TQ - 4, selh)
    for t in range(NT2):
        p2_tile(4 + t * TT2, TT2, False, t * TT2, selv)
    Sx.wait_all("sp", ["st_out"])
    Sx.wait_all("pool", ["st_out"])
    return finish()


def _unused():
    Sx.wait_all("sp", ["st_o"])
    Sx.wait_all("pool", ["pe", "act", "dve", "st_o"])
    return nc, stack


def _tile_w(w, ncols_pad=None):
    K, N = w.shape
    kc = K // 128
    return np.ascontiguousarray(w.reshape(kc, 128, N // 128, 128).transpose(2, 1, 0, 3).reshape(N // 128, 128, kc * 128))


def _vecT(v):
    return np.ascontiguousarray(v.reshape(-1, 128).T)


def _consts(TT):
    c = np.zeros((128, 2176), np.float32)
    c[:, 0:128] = np.eye(128, dtype=np.float32)
    s = np.arange(128) % 64
    j = np.arange(128)
    m = np.where(j[None, :] < 64, s[:, None] < j[None, :], s[:, None] <= (j[None, :] - 64)).astype(np.float32)
    c[:, 128:640] = np.tile(m, (1, 4))
    tl = (np.arange(64)[None, :] < np.arange(64)[:, None]).astype(np.float32)
    c[0:64, 640:1152] = np.tile(tl, (1, 8))
    c[0:64, 1152:1664] = np.tile(np.eye(64, dtype=np.float32), (1, 8))
    cm = np.ones(512, np.float32)
    cm[::64] = 0.0
    c[:, 1664:2176] = cm[None, :]
    return c


def prep_inputs(inp, S, phase2=True):
    f = lambda a: np.asarray(a, dtype=np.float32)
    TQ = S // 4
    TT = min(512, TQ)
    x = f(inp["x"])[:, :S]
    w_in = f(inp["w_in"])[0]
    NSH = 3 * DR + 96 + 96 + 256
    mu = f(inp["mu_shift"])[0]
    shared = {}
    shared["wada"] = _tile_w(f(inp["w_ada"])[0])
    if phase2:
        wc_cols = w_in[:, NSH:NSH + 3 * DR]
        wg_cols = w_in[:, NSH + 3 * DR:]
        t_cb, t_cc, t_cx = (_tile_w(wc_cols[:, i * DR:(i + 1) * DR]) for i in range(3))
        shared["w2c"] = np.ascontiguousarray(np.stack([t_cb, t_cc, t_cx], axis=1).reshape(48, 128, KC * 128))
        t_ga, t_gb = _tile_w(wg_cols[:, :D]), _tile_w(wg_cols[:, D:])
        shared["w2g"] = np.ascontiguousarray(np.stack([t_ga, t_gb], axis=1).reshape(64, 128, KC * 128))
        shared["wor"] = _tile_w(f(inp["w_o_rwkv"])[0])
        shared["woc"] = _tile_w(f(inp["w_o_conv"])[0])
        shared["wout"] = _tile_w(f(inp["w_out"])[0])
        shared["wup"] = _tile_w(f(inp["w_ffn_up"])[0])
        shared["wdn"] = _tile_w(f(inp["w_ffn_down"])[0])
    cst = _consts(TT)
    shared["cst"] = cst
    gains = np.concatenate([_vecT(f(inp["norm1_gain"])[0]), _vecT(f(inp["norm2_gain"])[0]),
                            _vecT(f(inp["final_gain"]))], axis=1)
    shared["vec0"] = np.ascontiguousarray(np.concatenate([_vecT(f(inp["b_ada"])[0]), gains], axis=1))
    cwm = f(inp["conv_w_mix"])[0]
    cwf = f(inp["conv_w_ffn"])[0]
    maps = []
    for i in range(NCORE):
        b, j = i // 4, i % 4
        m = dict(shared)
        m["x"] = np.ascontiguousarray(x[b])
        x2 = np.zeros((4 + TQ, D), np.float32)
        x2[4:] = x[b, j * TQ:(j + 1) * TQ]
        if j > 0:
            x2[:4] = x[b, j * TQ - 4:j * TQ]
        if phase2:
            m["x2"] = x2
        m["cT"] = _vecT(f(inp["c"])[b])
        ch = slice(j * 512, (j + 1) * 512)
        cols = np.concatenate([np.arange(k * DR + j * 512, k * DR + (j + 1) * 512) for k in range(3)])
        w1 = np.zeros((D, 16 * 128), np.float32)
        w1[:, 0:1536] = w_in[:, cols]
        w1[:, 1536:1632] = w_in[:, 3 * DR:3 * DR + 96]
        w1[:, 1664:1760] = w_in[:, 3 * DR + 96:3 * DR + 192]
        w1[:, 1792:2048] = w_in[:, 3 * DR + 192:3 * DR + 448]
        m["w1"] = _tile_w(w1)
        mu1 = np.zeros(16 * 128, np.float32)
        mu1[0:1536] = mu[cols]
        mu1[1536:1632] = mu[3 * DR:3 * DR + 96]
        mu1[1664:1760] = mu[3 * DR + 96:3 * DR + 192]
        mu1[1792:2048] = mu[3 * DR + 192:3 * DR + 448]
        v1 = [_vecT(mu1)]
        for nm in ("w0", "a0", "k_k", "k_a"):
            v1.append(_vecT(f(inp[nm])[0][ch]))
        v1.append(_vecT(f(inp["r_k"])[0].reshape(-1)[ch]))
        m["vec1"] = np.ascontiguousarray(np.concatenate(v1, axis=1))
        m["lnx"] = np.ascontiguousarray(np.tile(np.concatenate([f(inp["lnx_w"])[0][ch], f(inp["lnx_b"])[0][ch]])[None, :],
                                                (128, 1)))
        m["wld"] = np.ascontiguousarray(f(inp["w_lora_decay"])[0][:, ch])
        m["wli"] = np.ascontiguousarray(f(inp["w_lora_iclr"])[0][:, ch])
        wg = f(inp["w_lora_gate"])[0][:, ch]
        m["wlg"] = np.ascontiguousarray(wg.reshape(2, 128, 512).transpose(1, 0, 2).reshape(128, 1024))
        v2 = np.zeros((128, 48 + 3 * NFF + 32), np.float32)
        for k in range(3):
            v2[:, 16 * k:16 * k + 16] = _vecT(cwm[k])
            v2[:, 48 + NFF * k:48 + NFF * (k + 1)] = _vecT(cwf[k])
        o = 48 + 3 * NFF
        v2[:, o + 2 * j + b] = 1.0
        if j > 0:
            v2[:, o + 8 + 2 * (j - 1) + b] = 1.0
            v2[:, o + 16] = 1.0
        m["vec2"] = v2
        maps.append(m)
    return maps


_CACHE = {}


def kernel(**inputs):
    S = 8192
    if S not in _CACHE:
        _CACHE[S] = build(S)
    nc, _ = _CACHE[S]
    maps = prep_inputs(inputs, S)
    res = run_bass_kernel_spmd(nc, maps, core_ids=list(range(NCORE)))
    out = np.zeros((2, S, D), np.float32)
    TQ = S // 4
    for i in range(NCORE):
        out[i // 4, (i % 4) * TQ:(i % 4 + 1) * TQ] = res.results[i]["out"]
    return out
```
